# Optimizing a Trainium2 kernel written in Bass

```python
import jax, jax.numpy as jnp
from jax import lax
import numpy as np

D_MODEL = 1024
BATCH = 16
SEQ = 2048
DEPTH = 2

CHUNK = 64
M_HEADS = 4
M_HEAD_DIM = 256
M_WIDTH = M_HEADS * M_HEAD_DIM
M_CONV = 4
A_HEADS = 16
A_KV_HEADS = 4
A_HEAD_DIM = 64
A_Q_WIDTH = A_HEADS * A_HEAD_DIM
A_KV_WIDTH = A_KV_HEADS * A_HEAD_DIM
WINDOW = 128
A_PREV_CHUNKS = -(-(WINDOW - 1) // CHUNK)
ROPE_THETA = 10000.0
D_FF = 2816
N_IN = 3 * M_WIDTH + 2 * M_HEADS + A_Q_WIDTH + 2 * A_KV_WIDTH
EPS = 1e-6

kernel_name = 'hybrid_mlstm_swa_macaron_adaln'


def rms_norm(x, w):
    xf = x.astype(jnp.float32)
    y = xf * lax.rsqrt(jnp.mean(xf * xf, axis=-1, keepdims=True) + EPS)
    return (y * w.astype(jnp.float32)).astype(x.dtype)


def modulate(x, norm_w, shift, scale):
    return rms_norm(x, norm_w) * (1 + scale) + shift


def swiglu(h, w_in, w_out):
    a, g = jnp.split(h @ w_in, 2, axis=-1)
    return (jax.nn.silu(a) * g) @ w_out


def rope(x, pos):
    d = x.shape[-1]
    inv_freq = ROPE_THETA ** (-jnp.arange(0, d, 2, dtype=jnp.float32) / d)
    ang = pos.astype(jnp.float32)[..., None] * inv_freq
    cos, sin = jnp.cos(ang)[:, :, None, :], jnp.sin(ang)[:, :, None, :]
    xf = x.astype(jnp.float32)
    x1, x2 = xf[..., : d // 2], xf[..., d // 2:]
    return jnp.concatenate([x1 * cos - x2 * sin, x2 * cos + x1 * sin], axis=-1).astype(x.dtype)


def _mlstm_chunk(carry, xs):
    C, n, m = carry
    q, k, v, li, lf = xs
    L = q.shape[2]
    b = jnp.cumsum(lf, axis=-1)
    causal = jnp.arange(L)[:, None] >= jnp.arange(L)[None, :]
    d = jnp.where(causal, b[..., :, None] - b[..., None, :] + li[..., None, :], -jnp.inf)
    a = b + m[..., None]
    m_t = jnp.maximum(a, jnp.max(d, axis=-1))
    w_intra = jnp.exp(d - m_t[..., None])
    w_inter = jnp.exp(a - m_t)
    s = jnp.einsum('bhtd,bhsd->bhts', q, k) * w_intra
    num = jnp.einsum('bhts,bhsv->bhtv', s, v) + w_inter[..., None] * jnp.einsum('bhtd,bhdv->bhtv', q, C)
    den = jnp.sum(s, axis=-1) + w_inter * jnp.einsum('bhtd,bhd->bht', q, n)
    h = num / jnp.maximum(jnp.abs(den), jnp.exp(-m_t))[..., None]
    g = b[..., -1:] - b + li
    a_end = b[..., -1] + m
    m_new = jnp.maximum(a_end, jnp.max(g, axis=-1))
    wg = jnp.exp(g - m_new[..., None])
    decay = jnp.exp(a_end - m_new)
    kw = k * wg[..., None]
    C_new = decay[..., None, None] * C + jnp.einsum('bhsd,bhsv->bhdv', kw, v)
    n_new = decay[..., None] * n + jnp.sum(kw, axis=2)
    return (C_new, n_new, m_new), h


def mlstm_branch(u, v, o_pre, i_pre, f_pre, conv_w, conv_b, wq, wk, out_norm, skip):
    B, S, _ = u.shape
    nc = S // CHUNK
    uc = lax.conv_general_dilated(u, conv_w[:, None, :], window_strides=(1,),
                                  padding=[(M_CONV - 1, 0)],
                                  dimension_numbers=('NWC', 'WIO', 'NWC'),
                                  feature_group_count=M_WIDTH)
    ua = jax.nn.silu(uc + conv_b)
    uh = ua.reshape(B, S, M_HEADS, M_HEAD_DIM)
    q = jnp.einsum('bshd,hde->bshe', uh, wq)
    k = jnp.einsum('bshd,hde->bshe', uh, wk) * (M_HEAD_DIM ** -0.5)
    vh = v.reshape(B, S, M_HEADS, M_HEAD_DIM)

    def chunks4(t):
        return t.astype(jnp.float32).reshape(B, nc, CHUNK, M_HEADS, -1).transpose(1, 0, 3, 2, 4)

    def chunks3(t):
        return t.astype(jnp.float32).reshape(B, nc, CHUNK, M_HEADS).transpose(1, 0, 3, 2)

    li = chunks3(i_pre)
    lf = chunks3(jax.nn.log_sigmoid(f_pre.astype(jnp.float32)))
    carry0 = (jnp.zeros((B, M_HEADS, M_HEAD_DIM, M_HEAD_DIM), jnp.float32),
              jnp.zeros((B, M_HEADS, M_HEAD_DIM), jnp.float32),
              jnp.zeros((B, M_HEADS), jnp.float32))
    _, hs = lax.scan(_mlstm_chunk, carry0, (chunks4(q), chunks4(k), chunks4(vh), li, lf))
    h = hs.transpose(1, 0, 3, 2, 4).reshape(B, S, M_HEADS, M_HEAD_DIM)
    h = h * lax.rsqrt(jnp.mean(h * h, axis=-1, keepdims=True) + EPS)
    h = (h.reshape(B, S, M_WIDTH) * out_norm.astype(jnp.float32)).astype(u.dtype)
    return jax.nn.sigmoid(o_pre) * (h + skip * ua)


def swa_branch(q, k, v, pos, q_norm, k_norm, sinks):
    B, S, _ = q.shape
    nc = S // CHUNK
    G = A_HEADS // A_KV_HEADS
    q = rope(rms_norm(q.reshape(B, S, A_HEADS, A_HEAD_DIM), q_norm), pos)
    k = rope(rms_norm(k.reshape(B, S, A_KV_HEADS, A_HEAD_DIM), k_norm), pos)
    v = v.reshape(B, S, A_KV_HEADS, A_HEAD_DIM)
    qb = q.reshape(B, nc, CHUNK, A_KV_HEADS, G, A_HEAD_DIM)

    def band(t):
        t = t.reshape(B, nc, CHUNK, A_KV_HEADS, A_HEAD_DIM)
        tp = jnp.pad(t, ((0, 0), (A_PREV_CHUNKS, 0), (0, 0), (0, 0), (0, 0)))
        return jnp.concatenate([tp[:, j:j + nc] for j in range(A_PREV_CHUNKS + 1)], axis=2)

    kb, vb = band(k), band(v)
    key_chunk = jnp.arange(nc)[:, None] - A_PREV_CHUNKS + jnp.arange(A_PREV_CHUNKS + 1)[None, :]
    valid = jnp.repeat(key_chunk >= 0, CHUNK, axis=1)
    s = jnp.einsum('bnqhgd,bnkhd->bnhgqk', qb, kb).astype(jnp.float32) * (A_HEAD_DIM ** -0.5)
    s = jnp.where(valid[None, :, None, None, None, :], s, -jnp.inf)
    sink = jnp.broadcast_to(sinks.astype(jnp.float32).reshape(1, 1, A_KV_HEADS, G, 1, 1),
                            s.shape[:-1] + (1,))
    p = jax.nn.softmax(jnp.concatenate([s, sink], axis=-1), axis=-1)[..., :-1]
    o = jnp.einsum('bnhgqk,bnkhd->bnqhgd', p.astype(v.dtype), vb)
    return o.reshape(B, S, A_Q_WIDTH)


def hybrid_mixer(h, pos, w_in, m_gate_b, m_conv_w, m_conv_b, m_wq, m_wk, m_out_norm, m_skip,
                 a_q_norm, a_k_norm, a_sinks, proj_a, proj_b, merge_w, merge_b, w_out):
    sizes = [M_WIDTH, M_WIDTH, M_WIDTH, M_HEADS, M_HEADS, A_Q_WIDTH, A_KV_WIDTH, A_KV_WIDTH]
    offs = [int(o) for o in np.cumsum(sizes)[:-1]]
    u, vm, om, im, fm, qa, ka, va = jnp.split(h @ w_in, offs, axis=-1)
    im = im + m_gate_b[:M_HEADS]
    fm = fm + m_gate_b[M_HEADS:]
    ya = mlstm_branch(u, vm, om, im, fm, m_conv_w, m_conv_b, m_wq, m_wk, m_out_norm, m_skip)
    yb = swa_branch(qa, ka, va, pos, a_q_norm, a_k_norm, a_sinks)
    ga, gb = jnp.split(jax.nn.sigmoid(h @ merge_w + merge_b), 2, axis=-1)
    merged = ga * (ya @ proj_a) + gb * (yb @ proj_b)
    return merged @ w_out


def setup_inputs(seed: int = 0) -> dict:
    key = jax.random.key(seed)
    ks = jax.random.split(key, 32)
    L, D, F = DEPTH, D_MODEL, D_FF
    nrm = jax.random.normal

    def gain(k, shape):
        return 1.0 + 0.02 * nrm(k, shape, jnp.float32)

    offsets = jax.random.randint(ks[2], (BATCH,), 0, 64) * CHUNK
    positions = (offsets[:, None] + jnp.arange(SEQ, dtype=jnp.int32)[None, :]).astype(jnp.int32)
    m_gate_b = jnp.concatenate([0.1 * nrm(ks[8], (L, M_HEADS), jnp.float32),
                                jax.random.uniform(ks[9], (L, M_HEADS), jnp.float32, 3.0, 6.0)], axis=-1)
    return {
        'x': nrm(ks[0], (BATCH, SEQ, D), jnp.float32),
        'c': nrm(ks[1], (BATCH, D), jnp.float32),
        'positions': positions,
        'ada_w': 0.5 * D ** -0.5 * nrm(ks[3], (L, D, 9 * D), jnp.float32),
        'ada_b': 0.01 * nrm(ks[4], (L, 9 * D), jnp.float32),
        'ffn1_norm': gain(ks[5], (L, D)),
        'ffn1_w_in': D ** -0.5 * nrm(ks[6], (L, D, 2 * F), jnp.float32),
        'ffn1_w_out': F ** -0.5 * nrm(ks[7], (L, F, D), jnp.float32),
        'mix_norm': gain(ks[10], (L, D)),
        'mix_w_in': D ** -0.5 * nrm(ks[11], (L, D, N_IN), jnp.float32),
        'm_gate_b': m_gate_b,
        'm_conv_w': M_CONV ** -0.5 * nrm(ks[12], (L, M_CONV, M_WIDTH), jnp.float32),
        'm_conv_b': 0.01 * nrm(ks[13], (L, M_WIDTH), jnp.float32),
        'm_wq': M_HEAD_DIM ** -0.5 * nrm(ks[14], (L, M_HEADS, M_HEAD_DIM, M_HEAD_DIM), jnp.float32),
        'm_wk': M_HEAD_DIM ** -0.5 * nrm(ks[15], (L, M_HEADS, M_HEAD_DIM, M_HEAD_DIM), jnp.float32),
        'm_out_norm': gain(ks[16], (L, M_WIDTH)),
        'm_skip': gain(ks[17], (L, M_WIDTH)),
        'a_q_norm': gain(ks[18], (L, A_HEAD_DIM)),
        'a_k_norm': gain(ks[19], (L, A_HEAD_DIM)),
        'a_sinks': nrm(ks[20], (L, A_HEADS), jnp.float32),
        'proj_a': M_WIDTH ** -0.5 * nrm(ks[21], (L, M_WIDTH, D), jnp.float32),
        'proj_b': A_Q_WIDTH ** -0.5 * nrm(ks[22], (L, A_Q_WIDTH, D), jnp.float32),
        'merge_w': D ** -0.5 * nrm(ks[23], (L, D, 2 * D), jnp.float32),
        'merge_b': 0.01 * nrm(ks[24], (L, 2 * D), jnp.float32),
        'w_out': D ** -0.5 * nrm(ks[25], (L, D, D), jnp.float32),
        'ffn2_norm': gain(ks[26], (L, D)),
        'ffn2_w_in': D ** -0.5 * nrm(ks[27], (L, D, 2 * F), jnp.float32),
        'ffn2_w_out': F ** -0.5 * nrm(ks[28], (L, F, D), jnp.float32),
    }


def reference(x, c, positions, ada_w, ada_b, ffn1_norm, ffn1_w_in, ffn1_w_out, mix_norm, mix_w_in,
              m_gate_b, m_conv_w, m_conv_b, m_wq, m_wk, m_out_norm, m_skip, a_q_norm, a_k_norm,
              a_sinks, proj_a, proj_b, merge_w, merge_b, w_out, ffn2_norm, ffn2_w_in, ffn2_w_out):
    c_act = jax.nn.silu(c)
    for l in range(DEPTH):
        mod = (c_act @ ada_w[l] + ada_b[l])[:, None, :]
        sh1, sc1, g1, sh2, sc2, g2, sh3, sc3, g3 = jnp.split(mod, 9, axis=-1)
        h = modulate(x, ffn1_norm[l], sh1, sc1)
        x = x + 0.5 * g1 * swiglu(h, ffn1_w_in[l], ffn1_w_out[l])
        h = modulate(x, mix_norm[l], sh2, sc2)
        x = x + g2 * hybrid_mixer(h, positions, mix_w_in[l], m_gate_b[l], m_conv_w[l], m_conv_b[l],
                                  m_wq[l], m_wk[l], m_out_norm[l], m_skip[l], a_q_norm[l],
                                  a_k_norm[l], a_sinks[l], proj_a[l], proj_b[l], merge_w[l],
                                  merge_b[l], w_out[l])
        h = modulate(x, ffn2_norm[l], sh3, sc3)
        x = x + 0.5 * g3 * swiglu(h, ffn2_w_in[l], ffn2_w_out[l])
    return x
```

```python
import numpy as np
import concourse.bass as bass
import concourse.mybir as mybir
from concourse.bass_utils import run_bass_kernel_spmd

F32 = mybir.dt.float32
BF16 = mybir.dt.bfloat16
I32 = mybir.dt.int32
ALU = mybir.AluOpType
AF = mybir.ActivationFunctionType

NCORES = 8
L = 2
D = 1024
S = 2048
NBC = 2
T = NBC * S
FF = 2816
FC = 22
NIN = 4616
EPS = 1e-6
TWO_PI = 6.283185307179586
C1 = 6.28125
C2 = TWO_PI - C1

ADAB, N1, N2, N3, CW, CB, ONORM, SKIP, MB, QW, KW, SINK, GBI, GBF, SINK2, NPP = 0, 72, 80, 88, 96, 128, 136, 144, 152, 168, 169, 170, 178, 179, 180, 188
IDENT, ONES, BLK, ROT, MASK, SEL, INVF, MASK2, HSEL, NCONST = 0, 128, 256, 384, 512, 576, 1088, 1090, 1218, 1346

ENGS = ("sync", "scalar", "vector", "gpsimd", "tensor")


class FW:
    def __init__(self, nc, n_dma_sems=40, n_sw_sems=54):
        self.nc = nc
        self.prog = {e: [] for e in ENGS}
        self.cnt = {e: 0 for e in ENGS}
        self.waited = {e: {} for e in ENGS}
        self.state = {}
        self.n_dma = n_dma_sems
        self.dry = False
        self.dma_cnt = [0] * n_dma_sems
        self.dma_rr = 0
        self.sems = {}
        self._cm = []
        for e in ENGS:
            cm = nc.semaphore("s_" + e)
            self.sems[e] = cm.__enter__()
            self._cm.append(cm)
        for i in range(n_dma_sems):
            cm = nc.semaphore("d_%d" % i)
            self.sems["dma%d" % i] = cm.__enter__()
            self._cm.append(cm)
        self.n_sw = n_sw_sems
        self.sw_used = 0
        for i in range(n_sw_sems):
            cm = nc.semaphore("w_%d" % i)
            self.sems["sw%d" % i] = cm.__enter__()
            self._cm.append(cm)

    def _st(self, key):
        name, idx = key
        d = self.state.setdefault(name, {})
        return d.setdefault(idx, [None, {}])

    def _deps(self, key, write):
        name, idx = key
        d = self.state.setdefault(name, {})
        out = []
        ks = list(d.keys()) if idx is None else [k for k in (idx, None) if k in d]
        for k in ks:
            st = d[k]
            if st[0] is not None:
                out.append(st[0])
            if write:
                out.extend(st[1].items())
        return out

    def _emit_waits(self, eng, deps):
        need = {}
        for (s, v) in deps:
            if eng == "tensor" and s == "tensor":
                continue
            if need.get(s, 0) < v:
                need[s] = v
        for s, v in need.items():
            if self.waited[eng].get(s, 0) >= v:
                continue
            self.waited[eng][s] = v
            sem = self.sems[s]
            self.prog[eng].append(lambda e, sem=sem, v=v: e.wait_ge(sem, v))

    def _register(self, ev, reads, writes):
        s, v = ev
        for k in reads:
            r = self._st(k)[1]
            if r.get(s, 0) < v:
                r[s] = v
        for k in writes:
            name, idx = k
            if idx is None:
                self.state[name] = {None: [ev, {}]}
            else:
                st = self._st(k)
                st[0] = ev
                st[1] = {}

    def _alldeps(self, reads, writes):
        deps = []
        for k in reads:
            deps += self._deps(k, k[0].startswith("ps"))
        for k in writes:
            deps += self._deps(k, True)
        return deps

    def group(self, eng, fns, reads=(), writes=()):
        if self.dry:
            return
        self._emit_waits(eng, self._alldeps(reads, writes))
        for fn in fns[:-1]:
            self.prog[eng].append(lambda e, fn=fn: fn(e))
        self.cnt[eng] += 1
        v = self.cnt[eng]
        sem = self.sems[eng]
        fn = fns[-1]
        self.prog[eng].append(lambda e, fn=fn, sem=sem: fn(e).then_inc(sem, 1))
        self._register((eng, v), reads, writes)

    def op(self, eng, fn, reads=(), writes=()):
        self.group(eng, [fn], reads, writes)

    def dma(self, eng, out, in_, reads=(), writes=(), **kw):
        if self.dry:
            return
        deps = self._alldeps(reads, writes)
        if eng == "gpsimd":
            assert self.sw_used < self.n_sw, "out of software-DMA semaphores"
            sname = "sw%d" % self.sw_used
            self.sw_used += 1
            self._emit_waits(eng, deps)
            v = 16
        else:
            i = self.dma_rr
            self.dma_rr = (self.dma_rr + 1) % self.n_dma
            sname = "dma%d" % i
            if self.dma_cnt[i] > 0:
                deps.append((sname, self.dma_cnt[i]))
            self._emit_waits(eng, deps)
            self.dma_cnt[i] += 16
            v = self.dma_cnt[i]
        sem = self.sems[sname]
        self.prog[eng].append(
            lambda e, out=out, in_=in_, sem=sem, kw=kw: e.dma_start(out=out, in_=in_, **kw).then_inc(sem, 16))
        self._register((sname, v), reads, writes)

    def barrier(self):
        if self.dry:
            return
        for e in ENGS:
            deps = [(o, self.cnt[o]) for o in ENGS if o != e and self.cnt[o] > 0]
            deps += [("dma%d" % i, self.dma_cnt[i]) for i in range(self.n_dma) if self.dma_cnt[i] > 0]
            deps += [("sw%d" % i, 16) for i in range(self.sw_used)]
            self._emit_waits(e, deps)

    def finish(self):
        self.barrier()
        nc = self.nc
        with nc.Block() as block:
            for en in ENGS:
                prog = self.prog[en]

                def body(e, prog=prog):
                    for f in prog:
                        f(e)
                getattr(block, en)(body)
        for cm in reversed(self._cm):
            cm.__exit__(None, None, None)


class K:
    pass


def mm(k, out, okey, pairs, reads):
    n = len(pairs)
    fns = []
    for i, (l, r) in enumerate(pairs):
        fns.append(lambda e, l=l, r=r, i=i: e.matmul(out, lhsT=l, rhs=r, start=(i == 0), stop=(i == n - 1)))
    k.fw.group("tensor", fns, reads=reads, writes=[okey])


def mm_multi(k, groups, reads, writes):
    fns = []
    for (out, pairs) in groups:
        n = len(pairs)
        for i, (l, r) in enumerate(pairs):
            fns.append(lambda e, out=out, l=l, r=r, i=i, n=n: e.matmul(out, lhsT=l, rhs=r, start=(i == 0), stop=(i == n - 1)))
    k.fw.group("tensor", fns, reads=reads, writes=writes)


def bc(ap, shape):
    return ap.broadcast_to(shape)


class Alloc:
    def __init__(self, nc):
        self.nc = nc
        self.stack = []
        self.n = 0

    def sb(self, name, shape, dt):
        self.n += 1
        cm = self.nc.sbuf_tensor("sb%d_%s" % (self.n, name), shape, dt)
        t = cm.__enter__()
        self.stack.append(cm)
        return t

    def ps(self, name, shape, dt):
        cm = self.nc.psum_tensor(name, shape, dt)
        t = cm.__enter__()
        self.stack.append(cm)
        return t

    def mark(self):
        return len(self.stack)

    def release(self, mark):
        while len(self.stack) > mark:
            self.stack.pop().__exit__(None, None, None)


def load_cast(k, dst3, src3, key):
    k.fw.dma("gpsimd", dst3, src3, writes=[(key, None)])


def rms_h(k, xt, ntok, b, l, sub, hT, scratch_bf, lnv, rstd, hn, ps_bank, ps_key, xkey="xt", hkey="hT", split_sq=False, part=0):
    fw = k.fw
    rk = "lnv" if rstd is lnv else "rstd"
    if part in (0, 1):
        _rms_stats(k, xt, ntok, scratch_bf, lnv, rstd, ps_bank, ps_key, xkey, split_sq, rk)
    if part == 1:
        return
    for kc in range(8):
        h_ = hn[kc % 2]
        fw.op("vector", lambda e, kc=kc, h_=h_: e.tensor_tensor(out=h_[:], in0=xt[:, kc, :], in1=rstd[:], op=ALU.mult),
              reads=[(xkey, kc), (rk, None)], writes=[("hn", kc % 2)])
        A = k.der[:, l, 3 * sub + 0, kc, b:b + 1]
        sh = k.der[:, l, 3 * sub + 1, kc, b:b + 1]
        fw.op("scalar", lambda e, kc=kc, h_=h_, A=A, sh=sh: e.activation(out=hT[:, kc, :], in_=h_[:], func=AF.Identity, scale=A, bias=sh),
              reads=[("hn", kc % 2), ("der", None)], writes=[(hkey, kc)])


def _rms_stats(k, xt, ntok, scratch_bf, lnv, rstd, ps_bank, ps_key, xkey, split_sq, rk):
    fw = k.fw
    if split_sq:
        for kc in range(8):
            fw.op("scalar", lambda e, kc=kc: e.activation(out=scratch_bf[0][:, kc, :], in_=xt[:, kc, :], func=AF.Square),
                  reads=[(xkey, kc)], writes=[(scratch_bf[1], kc)])
    else:
        fw.op("scalar", lambda e: e.activation(out=scratch_bf[0][:, 0:8, :], in_=xt[:], func=AF.Square),
              reads=[(xkey, None)], writes=[(scratch_bf[1], i) for i in range(8)])
    if split_sq:
        for kc in range(8):
            fw.op("tensor", lambda e, kc=kc: e.matmul(ps_bank[:, 0:ntok], lhsT=k.ones_bf[:], rhs=scratch_bf[0][:, kc, :], start=(kc == 0), stop=(kc == 7)),
                  reads=[(scratch_bf[1], kc), ("ones_bf", None)], writes=[ps_key])
    else:
        mm(k, ps_bank[:, 0:ntok], ps_key, [(k.ones_bf[:], scratch_bf[0][:, kc, :]) for kc in range(8)],
           reads=[(scratch_bf[1], i) for i in range(8)])
    fw.op("scalar", lambda e: e.activation(out=lnv[:], in_=ps_bank[:, 0:ntok], func=AF.Ln, scale=1.0 / D, bias=EPS),
          reads=[ps_key], writes=[("lnv", None)])
    fw.op("scalar", lambda e: e.activation(out=rstd[:], in_=lnv[:], func=AF.Exp, scale=-0.5),
          reads=[("lnv", None)], writes=[(rk, None)])


def load_consts(k):
    cst = k.al.sb("consts", [128, NCONST], F32)
    k.fw.dma("sync", cst[:], k.d["consts"], writes=[("consts", None)])
    return cst


def phase_setup(k):
    fw, nc, al = k.fw, k.nc, k.al
    k.pp = al.sb("pp", [128, L, NPP], F32)
    k.der = al.sb("der", [128, L, 9, 8, 2], F32)
    k.ident_bf = al.sb("ident_bf", [128, 128], BF16)
    k.ones_bf = al.sb("ones_bf", [128, 128], BF16)
    k.blk_bf = al.sb("blk_bf", [128, 128], BF16)
    k.nbf = al.sb("nbf", [4, L], F32)
    k.esink = al.sb("esink", [128, L, 8], F32)
    k.ps = [al.ps("psb%d" % i, [128, 512], F32) for i in range(8)]
    fw.dma("sync", k.pp[:], k.d["pp"], writes=[("pp", None)])
    mark = al.mark()
    cst = load_consts(k)
    invf = cst[:, INVF:INVF + 1]
    fw.op("vector", lambda e: e.tensor_copy(out=k.ident_bf[:], in_=cst[:, IDENT:IDENT + 128]), reads=[("consts", None)], writes=[("ident_bf", None)])
    fw.op("vector", lambda e: e.tensor_copy(out=k.ones_bf[:], in_=cst[:, ONES:ONES + 128]), reads=[("consts", None)], writes=[("ones_bf", None)])
    fw.op("vector", lambda e: e.tensor_copy(out=k.blk_bf[:], in_=cst[:, BLK:BLK + 128]), reads=[("consts", None)], writes=[("blk_bf", None)])
    for l in range(L):
        fw.op("vector", lambda e, l=l: e.tensor_scalar(out=k.nbf[:, l:l + 1], in0=k.pp[0:4, l, GBF:GBF + 1], scalar1=-1.0, scalar2=None, op0=ALU.mult),
              reads=[("pp", None)], writes=[("nbf", None)])
        fw.op("scalar", lambda e, l=l: e.activation(out=k.esink[:, l, :], in_=k.pp[:, l, SINK:SINK + 8], func=AF.Exp),
              reads=[("pp", None)], writes=[("esink", None)])
    cT = al.sb("cT", [128, 8, 2], F32)
    cact = al.sb("cact", [128, 8, 2], F32)
    stage = al.sb("adaw_st", [128, 2, 8, 512], F32)
    modT = al.sb("modT", [128, 72, 2], F32)
    fw.dma("sync", cT[:], k.d["cT"], writes=[("cT", None)])
    fw.op("scalar", lambda e: e.activation(out=cact[:], in_=cT[:], func=AF.Silu), reads=[("cT", None)], writes=[("cact", None)])
    ps7 = k.ps[7]
    for l in range(L):
        for g in range(18):
            fw.dma("sync", stage[:, g % 2], k.d["ada_w"][l, :, :, g * 512:(g + 1) * 512], writes=[("adaw_st", g % 2)])
            for jj in range(4):
                j = g * 4 + jj
                mm(k, ps7[:, 2 * j:2 * j + 2], ("ps7", None),
                   [(stage[:, g % 2, kc, jj * 128:(jj + 1) * 128], cact[:, kc, :]) for kc in range(8)],
                   reads=[("adaw_st", g % 2), ("cact", None)])
        fw.op("vector", lambda e, l=l: e.tensor_tensor(out=modT[:], in0=ps7[:, 0:144].rearrange("p (j b) -> p j b", b=2),
                                                   in1=bc(k.pp[:, l, ADAB:ADAB + 72].rearrange("p (j o) -> p j o", o=1), [128, 72, 2]), op=ALU.add),
              reads=[("ps7", None), ("pp", None)], writes=[("modT", None)])
        for sub in range(3):
            ncol = (N1, N2, N3)[sub]
            sh = modT[:, (3 * sub) * 8:(3 * sub) * 8 + 8, :]
            sc = modT[:, (3 * sub + 1) * 8:(3 * sub + 1) * 8 + 8, :]
            g_ = modT[:, (3 * sub + 2) * 8:(3 * sub + 2) * 8 + 8, :]
            nrm = bc(k.pp[:, l, ncol:ncol + 8].rearrange("p (j o) -> p j o", o=1), [128, 8, 2])
            fw.op("vector", lambda e, l=l, sub=sub, sc=sc, nrm=nrm: e.scalar_tensor_tensor(out=k.der[:, l, 3 * sub + 0], in0=sc, scalar=1.0, in1=nrm, op0=ALU.add, op1=ALU.mult),
                  reads=[("modT", None), ("pp", None)], writes=[("der", None)])
            fw.op("vector", lambda e, l=l, sub=sub, sh=sh: e.tensor_copy(out=k.der[:, l, 3 * sub + 1], in_=sh),
                  reads=[("modT", None)], writes=[("der", None)])
            gs = 1.0 if sub == 1 else 0.5
            fw.op("vector", lambda e, l=l, sub=sub, g_=g_, gs=gs: e.tensor_scalar(out=k.der[:, l, 3 * sub + 2], in0=g_, scalar1=gs, scalar2=None, op0=ALU.mult),
                  reads=[("modT", None)], writes=[("der", None)])
    posi = al.sb("posi", [128, S], I32)
    ang = al.sb("ang", [128, S], F32)
    t0 = al.sb("rt0", [128, S], F32)
    ni = al.sb("rni", [128, S], I32)
    nf = al.sb("rnf", [128, S], F32)
    sn = al.sb("rsn", [128, S], F32)
    cs = al.sb("rcs", [128, S], F32)
    for b in range(NBC):
        fw.dma("sync", posi[:], k.d["pos"][b:b + 1, :].broadcast_to([128, S]), writes=[("posi", None)])
        fw.op("vector", lambda e: e.tensor_copy(out=ang[:], in_=posi[:]), reads=[("posi", None)], writes=[("ang", None)])
        fw.op("vector", lambda e: e.tensor_scalar(out=ang[:], in0=ang[:], scalar1=invf, scalar2=None, op0=ALU.mult),
              reads=[("ang", None), ("consts", None)], writes=[("ang", None)])
        fw.op("vector", lambda e: e.tensor_scalar(out=t0[:], in0=ang[:], scalar1=1.0 / TWO_PI, scalar2=None, op0=ALU.mult),
              reads=[("ang", None)], writes=[("rt0", None)])
        fw.op("vector", lambda e: e.tensor_copy(out=ni[:], in_=t0[:]), reads=[("rt0", None)], writes=[("rni", None)])
        fw.op("vector", lambda e: e.tensor_copy(out=nf[:], in_=ni[:]), reads=[("rni", None)], writes=[("rnf", None)])
        fw.op("vector", lambda e: e.scalar_tensor_tensor(out=t0[:], in0=nf[:], scalar=-C1, in1=ang[:], op0=ALU.mult, op1=ALU.add),
              reads=[("rnf", None), ("ang", None)], writes=[("rt0", None)])
        fw.op("vector", lambda e: e.scalar_tensor_tensor(out=t0[:], in0=nf[:], scalar=-C2, in1=t0[:], op0=ALU.mult, op1=ALU.add),
              reads=[("rnf", None), ("rt0", None)], writes=[("rt0", None)])
        fw.op("vector", lambda e: e.tensor_scalar(out=t0[:], in0=t0[:], scalar1=3.1415925, scalar2=-3.1415925, op0=ALU.min, op1=ALU.max),
              reads=[("rt0", None)], writes=[("rt0", None)])
        fw.op("scalar", lambda e: e.activation(out=sn[:], in_=t0[:], func=AF.Sin), reads=[("rt0", None)], writes=[("rsn", None)])
        fw.op("scalar", lambda e: e.activation(out=cs[:], in_=t0[:], func=AF.Sin, scale=0.5), reads=[("rt0", None)], writes=[("rcs", None)])
        fw.op("vector", lambda e: e.tensor_tensor(out=cs[:], in0=cs[:], in1=cs[:], op=ALU.mult), reads=[("rcs", None)], writes=[("rcs", None)])
        fw.op("vector", lambda e: e.tensor_scalar(out=cs[:], in0=cs[:], scalar1=-2.0, scalar2=1.0, op0=ALU.mult, op1=ALU.add),
              reads=[("rcs", None)], writes=[("rcs", None)])
        fw.dma("sync", k.d["sinT"][:, b * S:(b + 1) * S], sn[:], reads=[("rsn", None)], writes=[("sinTd", None)])
        fw.dma("sync", k.d["cosT"][:, b * S:(b + 1) * S], cs[:], reads=[("rcs", None)], writes=[("cosTd", None)])
    fw.barrier()
    al.release(mark)


def phase_ffn(k, l, which, src_tok, dst_tok, src_ap=None, dst_ap=None):
    fw, nc, al, ps = k.fw, k.nc, k.al, k.ps
    sub = 0 if which == 1 else 2
    mark = al.mark()
    NT = 512
    W1 = al.sb("W1", [128, 8, 2 * FF], BF16)
    W2 = al.sb("W2", [128, FC, D], BF16)
    w1d = k.d["f%d_win" % which][l].rearrange("p (a n) -> p a n", n=2 * FF)
    w1d4 = w1d.rearrange("p a (two n) -> p a two n", two=2)
    W14 = W1[:].rearrange("p a (two n) -> p a two n", two=2)
    for gq in range(6):
        c0, c1 = gq * 512, min((gq + 1) * 512, FF)
        fw.dma("gpsimd", W14[:, :, :, c0:c1], w1d4[:, :, :, c0:c1], writes=[("W1a", gq), ("W1g", gq)])
    load_cast(k, W2[:].rearrange("p a n -> p (a n)").rearrange("p (a e) -> p a e", e=2048),
              k.d["f%d_wout" % which][l].rearrange("p (a e) -> p a e", e=2048), "W2")
    xts = [al.sb("xt%d" % i, [128, 8, NT], F32) for i in range(2)]
    hT = al.sb("hT", [128, 8, NT], BF16)
    actT = al.sb("actT", [128, FC, NT], BF16)
    lnv = al.sb("lnv", [128, NT], F32)
    rstd = lnv
    srcx = src_ap if src_ap is not None else k.d["xT"]
    hn = [al.sb("hn%d" % i, [128, NT], F32) for i in range(2)]
    sa = [al.sb("sa%d" % i, [128, NT], F32) for i in range(2)]
    xin = al.sb("xin", [128, 2, D], F32) if (src_tok or dst_tok) else None
    ev = 0
    for it in range(T // NT):
        b = (it * NT) // S
        tok0 = it * NT
        if src_tok:
            for tb in range(4):
                fw.dma("sync", xin[:, tb % 2, :], k.d["x"][tok0 + tb * 128:tok0 + (tb + 1) * 128, :], writes=[("xin", tb % 2)])
                for half in range(2):
                    bank = (tb * 2 + half) % 4
                    fns = [lambda e, q=q, bank=bank, half=half, tb=tb: e.transpose(out=ps[bank][:, q * 128:(q + 1) * 128], in_=xin[:, tb % 2, (half * 4 + q) * 128:(half * 4 + q + 1) * 128], identity=k.ident_f)
                           for q in range(4)]
                    fw.group("tensor", fns, reads=[("xin", tb % 2), ("consts", None)], writes=[("ps%d" % bank, None)])
                    src = ps[bank][:].rearrange("p (q t) -> p q t", t=128)
                    dst = xt[:, half * 4:(half + 1) * 4, tb * 128:(tb + 1) * 128]
                    eng = "vector" if ev % 2 == 0 else "scalar"
                    ev += 1
                    if eng == "vector":
                        fw.op("vector", lambda e, src=src, dst=dst: e.tensor_copy(out=dst, in_=src), reads=[("ps%d" % bank, None)], writes=[("xt", half * 4 + q) for q in range(4)])
                    else:
                        fw.op("scalar", lambda e, src=src, dst=dst: e.activation(out=dst, in_=src, func=AF.Copy), reads=[("ps%d" % bank, None)], writes=[("xt", half * 4 + q) for q in range(4)])
        else:
            xk = "xt%d" % (it % 2)
            xt = xts[it % 2]
            if it == 0:
                for kc in range(8):
                    fw.dma("sync", xt[:, kc, :], srcx[:, kc, tok0:tok0 + NT], writes=[(xk, kc)])
            if it + 1 < T // NT:
                for kc in range(8):
                    fw.dma("sync", xts[(it + 1) % 2][:, kc, :], srcx[:, kc, tok0 + NT:tok0 + 2 * NT], writes=[("xt%d" % ((it + 1) % 2), kc)])
        if it == 0:
            rms_h(k, xt, NT, b, l, sub, hT, (hT, "hT"), lnv, rstd, hn, ps[6], ("ps6", None), xkey=xk, split_sq=True, part=1)
        rms_h(k, xt, NT, b, l, sub, hT, (hT, "hT"), lnv, rstd, hn, ps[6], ("ps6", None), xkey=xk, split_sq=True, part=2)
        for j in range(FC):
            pa_, pg_ = ps[j % 2], ps[2 + j % 2]
            if j == 0:
                for kc in range(8):
                    fw.op("tensor", lambda e, kc=kc, pa_=pa_: e.matmul(pa_[:], lhsT=W1[:, kc, 0:128], rhs=hT[:, kc, :], start=(kc == 0), stop=(kc == 7)),
                          reads=[("W1a", 0), ("hT", kc)], writes=[("ps0", None)])
                    fw.op("tensor", lambda e, kc=kc, pg_=pg_: e.matmul(pg_[:], lhsT=W1[:, kc, FF:FF + 128], rhs=hT[:, kc, :], start=(kc == 0), stop=(kc == 7)),
                          reads=[("W1g", 0), ("hT", kc)], writes=[("ps2", None)])
            else:
                mm(k, pa_[:], ("ps%d" % (j % 2), None), [(W1[:, kc, j * 128:(j + 1) * 128], hT[:, kc, :]) for kc in range(8)], reads=[("W1a", j // 4), ("hT", None)])
                mm(k, pg_[:], ("ps%d" % (2 + j % 2), None), [(W1[:, kc, FF + j * 128:FF + (j + 1) * 128], hT[:, kc, :]) for kc in range(8)], reads=[("W1g", j // 4), ("hT", None)])
            s_ = sa[j % 2]
            fw.op("scalar", lambda e, s_=s_, pa_=pa_: e.activation(out=s_[:], in_=pa_[:], func=AF.Silu), reads=[("ps%d" % (j % 2), None)], writes=[("sa", j % 2)])
            fw.op("vector", lambda e, s_=s_, pg_=pg_, j=j: e.tensor_tensor(out=actT[:, j, :], in0=s_[:], in1=pg_[:], op=ALU.mult),
                  reads=[("sa", j % 2), ("ps%d" % (2 + j % 2), None)], writes=[("actT", j)])
        for o in range(8):
            py = ps[4 + o % 2]
            if o == 0:
                for j in range(FC):
                    fw.op("tensor", lambda e, j=j, py=py: e.matmul(py[:], lhsT=W2[:, j, 0:128], rhs=actT[:, j, :], start=(j == 0), stop=(j == FC - 1)),
                          reads=[("W2", None), ("actT", j)], writes=[("ps4", None)])
            else:
                mm(k, py[:], ("ps%d" % (4 + o % 2), None), [(W2[:, j, o * 128:(o + 1) * 128], actT[:, j, :]) for j in range(FC)],
                   reads=[("W2", None)] + [("actT", j) for j in range(FC)])
            G = k.der[:, l, 3 * sub + 2, o, b:b + 1]
            fw.op("vector", lambda e, py=py, o=o, G=G, xt=xt: e.scalar_tensor_tensor(out=xt[:, o, :], in0=py[:], scalar=G, in1=xt[:, o, :], op0=ALU.mult, op1=ALU.add),
                  reads=[("ps%d" % (4 + o % 2), None), (xk, o), ("der", None)], writes=[(xk, o)])
            if not dst_tok:
                fw.dma("scalar", (dst_ap if dst_ap is not None else k.d["xT"])[:, o, tok0:tok0 + NT], xt[:, o, :], reads=[(xk, o)], writes=[("xTd", o)])
            if o == 1 and it + 1 < T // NT:
                rms_h(k, xts[(it + 1) % 2], NT, b, l, sub, hT, (hT, "hT"), lnv, rstd, hn, ps[6], ("ps6", None), xkey="xt%d" % ((it + 1) % 2), split_sq=True, part=1)
        if dst_tok:
            for tb in range(4):
                for half in range(2):
                    bank = (tb * 2 + half) % 4
                    fns = [lambda e, q=q, bank=bank, half=half, tb=tb: e.transpose(out=ps[bank][:, q * 128:(q + 1) * 128], in_=xt[:, half * 4 + q, tb * 128:(tb + 1) * 128], identity=k.ident_f)
                           for q in range(4)]
                    fw.group("tensor", fns, reads=[("xt", half * 4 + q) for q in range(4)] + [("consts", None)], writes=[("ps%d" % bank, None)])
                    dst = xin[:, tb % 2, half * 512:(half + 1) * 512]
                    eng = "vector" if ev % 2 == 0 else "scalar"
                    ev += 1
                    if eng == "vector":
                        fw.op("vector", lambda e, dst=dst, bank=bank: e.tensor_copy(out=dst, in_=ps[bank][:]), reads=[("ps%d" % bank, None)], writes=[("xin", tb % 2)])
                    else:
                        fw.op("scalar", lambda e, dst=dst, bank=bank: e.activation(out=dst, in_=ps[bank][:], func=AF.Copy), reads=[("ps%d" % bank, None)], writes=[("xin", tb % 2)])
                fw.dma("sync", k.d["out"][tok0 + tb * 128:tok0 + (tb + 1) * 128, :], xin[:, tb % 2, :], reads=[("xin", tb % 2)], writes=[("outd", None)])
    fw.barrier()
    al.release(mark)


def count_steps(k, gen_fn, it):
    k.fw.dry = True
    n = sum(1 for _ in gen_fn(it))
    k.fw.dry = False
    return n


def run_pipelined(k, n, front, back):
    if not k.overlap:
        for it in range(n):
            for _ in front(it):
                pass
            for _ in back(it):
                pass
        return
    for _ in front(0):
        pass
    for it in range(n):
        gb = back(it)
        if it + 1 < n:
            nb = count_steps(k, back, it)
            nf = count_steps(k, front, it + 1)
            gf = front(it + 1)
            ib = jf = 0
            while ib < nb or jf < nf:
                if jf >= nf or (ib < nb and ib * nf <= jf * nb):
                    next(gb, None)
                    ib += 1
                else:
                    next(gf, None)
                    jf += 1
            for _ in gb:
                pass
            for _ in gf:
                pass
        else:
            for _ in gb:
                pass


def phase_mixA(k, l):
    fw, nc, al, ps = k.fw, k.nc, k.al, k.ps
    mark = al.mark()
    cst = load_consts(k)
    NT = 256
    CH = 128
    NCH = NT // CH
    mixw = k.d["mixw"][l]
    Wu = al.sb("Wu", [128, 8, 1024], BF16)
    Wvm = al.sb("Wvm", [128, 8, 1024], BF16)
    Wom = al.sb("Wom", [128, 8, 1024], BF16)
    Wif32 = al.sb("Wif32", [128, 8, 8], F32)
    Wif = al.sb("Wif", [128, 8, 8], BF16)
    Wq = al.sb("Wq", [128, 2, 4, 256], BF16)
    Wk = al.sb("Wk", [128, 2, 4, 256], BF16)
    load_cast(k, Wu[:], mixw[:, :, 0:1024], "Wu")
    fw.dma("sync", Wif32[:], mixw[:, :, 3072:3080], writes=[("Wif32", None)])
    fw.op("vector", lambda e: e.tensor_copy(out=Wif[:], in_=Wif32[:]), reads=[("Wif32", None)], writes=[("Wif", None)])
    load_cast(k, Wom[:], mixw[:, :, 2048:3072], "Wom")
    load_cast(k, Wvm[:], mixw[:, :, 1024:2048], "Wvm")
    load_cast(k, Wq[:].rearrange("p a h e -> p a (h e)"), k.d["wq"][l], "Wq")
    load_cast(k, Wk[:].rearrange("p a h e -> p a (h e)"), k.d["wk"][l], "Wk")
    xts = [al.sb("xt%d" % i, [128, 8, NT], F32) for i in range(2)]
    hT = al.sb("hT", [128, 8, NT], BF16)
    lnv = al.sb("lnv", [128, NT], F32)
    rstd = al.sb("rstd", [128, NT], F32)
    hn = [al.sb("hn%d" % i, [128, NT], F32) for i in range(2)]
    u_sb = al.sb("u_sb", [128, 8, NT + 3], F32)
    acc = [al.sb("acc%d" % i, [128, NT], F32) for i in range(2)]
    gi = al.sb("gi", [4, NT], F32)
    ef = al.sb("ef", [4, NT], F32)
    Bext = al.sb("Bext", [4, NT + 1], F32)
    Et = al.sb("Et", [4, NT], F32)
    Mext = al.sb("Mext", [4, NT + 1], F32)
    g1 = al.sb("g1", [4, NT], F32)
    g2 = al.sb("g2", [4, NT], F32)
    dect = al.sb("dect", [4, 4], F32)
    ones4 = al.sb("ones4", [4, NT], F32)
    H = []
    for i in range(2):
        H.append(dict(
            uaT=al.sb("uaT%d" % i, [128, 8, NT], BF16), sigo=al.sb("sigo%d" % i, [128, 8, NT], BF16),
            qT=al.sb("qT%d" % i, [128, 8, NT], BF16), kT=al.sb("kT%d" % i, [128, 8, NT], BF16),
            ktm=al.sb("ktm%d" % i, [128, 2, 4, 256], BF16), vx=al.sb("vx%d" % i, [128, 2, 4, 384], BF16),
            clamp_rep=al.sb("clamp_rep%d" % i, [128, 4, NT], F32), dec_rep=al.sb("dec_rep%d" % i, [128, 4, 4], F32),
            kscT=al.sb("kscT%d" % i, [128, 2, 4], F32)))
    Cx = al.sb("Cx", [128, 4, 2, 384], BF16)
    SMs = [al.sb("SM%d" % i, [128, 4, CH], BF16) for i in range(NCH)]
    hraw = al.sb("hraw", [128, 4, 2, NT], F32)
    den = al.sb("den", [128, CH], F32)
    rec = al.sb("rec", [128, CH], F32)
    sqh = [al.sb("sqh%d" % i, [128, 2, NT], BF16) for i in range(2)]
    lnh = al.sb("lnh", [128, NT], F32)
    rsh = [al.sb("rsh%d" % i, [128, NT], F32) for i in range(2)]
    tA = [al.sb("tA%d" % i, [128, NT], F32) for i in range(2)]
    tB = [al.sb("tB%d" % i, [128, NT], F32) for i in range(2)]
    tC = [al.sb("tC%d" % i, [128, NT], F32) for i in range(2)]
    yaT = al.sb("yaT", [128, 8, NT], BF16)
    fw.op("vector", lambda e: e.memset(ones4[:], 1.0), writes=[("ones4", None)])
    fw.op("vector", lambda e: e.memset(dect[:], 0.0), writes=[("dect", None)])
    pcw = lambda m, j: k.pp[:, l, CW + m * 4 + j:CW + m * 4 + j + 1]

    def front(it):
        par = it % 2
        hb = H[par]
        uaT, sigo, qT, kT, ktm, vx, clamp_rep, dec_rep, kscT = (hb[n] for n in ("uaT", "sigo", "qT", "kT", "ktm", "vx", "clamp_rep", "dec_rep", "kscT"))
        P = "%d" % par
        b = (it * NT) // S
        tok0 = it * NT
        seq_start = (tok0 % S) == 0
        if seq_start:
            fw.op("gpsimd", lambda e: e.memset(u_sb[:, :, 0:3], 0.0), writes=[("u_sb", None)])
            fw.op("vector", lambda e: e.memset(Bext[:, 0:1], 0.0), writes=[("Bext", None)])
            fw.op("vector", lambda e: e.memset(Mext[:, 0:1], 0.0), writes=[("Mext", None)])
        else:
            fw.op("vector", lambda e: e.tensor_copy(out=Bext[:, 0:1], in_=Bext[:, NT:NT + 1]), reads=[("Bext", None)], writes=[("Bext", None)])
            fw.op("vector", lambda e: e.tensor_copy(out=Mext[:, 0:1], in_=Mext[:, NT:NT + 1]), reads=[("Mext", None)], writes=[("Mext", None)])
        xt = xts[par]
        if it == 0:
            fw.dma("sync", xt[:], k.d["xT"][:, :, tok0:tok0 + NT], writes=[("xt" + P, None)])
        if it + 1 < T // NT:
            fw.dma("sync", xts[1 - par][:], k.d["xT"][:, :, tok0 + NT:tok0 + 2 * NT], writes=[("xt%d" % (1 - par), None)])
        yield
        rms_h(k, xt, NT, b, l, 1, hT, (hT, "hT"), lnv, rstd, hn, ps[0], ("ps0", None), xkey="xt" + P)
        fw.dma("sync", k.d["hTs"][:, :, tok0:tok0 + NT], hT[:], reads=[("hT", None)], writes=[("hTsd", it)])
        for _ in range(3):
            yield
        mm(k, ps[0][0:4, 0:NT], ("ps0", None), [(Wif[:, kc, 0:4], hT[:, kc, :]) for kc in range(8)], reads=[("Wif", None), ("hT", None)])
        fw.op("scalar", lambda e: e.activation(out=gi[:], in_=ps[0][0:4, 0:NT], func=AF.Identity, bias=k.pp[0:4, l, GBI:GBI + 1], scale=1.0),
              reads=[("ps0", None), ("pp", None)], writes=[("gi", None)])
        mm(k, ps[0][0:4, 0:NT], ("ps0", None), [(Wif[:, kc, 4:8], hT[:, kc, :]) for kc in range(8)], reads=[("Wif", None), ("hT", None)])
        fw.op("scalar", lambda e: e.activation(out=ef[:], in_=ps[0][0:4, 0:NT], func=AF.Exp, bias=k.nbf[:, l:l + 1], scale=-1.0),
              reads=[("ps0", None), ("nbf", None)], writes=[("ef", None)])
        fw.op("scalar", lambda e: e.activation(out=ef[:], in_=ef[:], func=AF.Ln, bias=1.0, scale=1.0), reads=[("ef", None)], writes=[("ef", None)])
        fw.op("vector", lambda e: e.tensor_tensor_scan(out=Bext[:, 1:NT + 1], data0=ones4[:], data1=ef[:], initial=Bext[:, 0:1], op0=ALU.mult, op1=ALU.subtract),
              reads=[("ones4", None), ("ef", None), ("Bext", None)], writes=[("Bext", None)])
        fw.op("vector", lambda e: e.tensor_tensor(out=Et[:], in0=gi[:], in1=Bext[:, 1:NT + 1], op=ALU.subtract),
              reads=[("gi", None), ("Bext", None)], writes=[("Et", None)])
        fw.op("vector", lambda e: e.tensor_tensor_scan(out=Mext[:, 1:NT + 1], data0=ones4[:], data1=Et[:], initial=Mext[:, 0:1], op0=ALU.mult, op1=ALU.max),
              reads=[("ones4", None), ("Et", None), ("Mext", None)], writes=[("Mext", None)])
        R3 = Mext[:, 0:NT].rearrange("p (c q) -> p c q", q=CH)[:, :, 0:1]
        Rn3 = Mext[:, 1:NT + 1].rearrange("p (c q) -> p c q", q=CH)[:, :, CH - 1:CH]
        fw.op("vector", lambda e: e.tensor_tensor(out=g1[:].rearrange("p (c q) -> p c q", q=CH), in0=Et[:].rearrange("p (c q) -> p c q", q=CH), in1=bc(R3, [4, NCH, CH]), op=ALU.subtract),
              reads=[("Et", None), ("Mext", None)], writes=[("g1", None)])
        fw.op("scalar", lambda e: e.activation(out=g1[:], in_=g1[:], func=AF.Exp), reads=[("g1", None)], writes=[("g1", None)])
        fw.op("vector", lambda e: e.tensor_tensor(out=g2[:].rearrange("p (c q) -> p c q", q=CH), in0=Bext[:, 1:NT + 1].rearrange("p (c q) -> p c q", q=CH), in1=bc(R3, [4, NCH, CH]), op=ALU.add),
              reads=[("Bext", None), ("Mext", None)], writes=[("g2", None)])
        fw.op("scalar", lambda e: e.activation(out=g2[:], in_=g2[:], func=AF.Exp, scale=-1.0), reads=[("g2", None)], writes=[("g2", None)])
        fw.op("vector", lambda e: e.tensor_tensor(out=dect[:, 0:NCH].rearrange("p (c o) -> p c o", o=1), in0=R3, in1=Rn3, op=ALU.subtract),
              reads=[("Mext", None)], writes=[("dect", None)])
        fw.op("scalar", lambda e: e.activation(out=dect[:, 0:NCH], in_=dect[:, 0:NCH], func=AF.Exp), reads=[("dect", None)], writes=[("dect", None)])
        yield
        for mp in range(4):
            bank = mp % 2
            mm_multi(k, [(ps[bank][:, i * NT:(i + 1) * NT], [(Wu[:, kc, (2 * mp + i) * 128:(2 * mp + i + 1) * 128], hT[:, kc, :]) for kc in range(8)]) for i in range(2)],
                     reads=[("Wu", None), ("hT", None)], writes=[("ps%d" % bank, None)])
            fw.op("scalar", lambda e, mp=mp, bank=bank: e.activation(out=u_sb[:, 2 * mp:2 * mp + 2, 3:NT + 3], in_=ps[bank][:].rearrange("p (i t) -> p i t", t=NT), func=AF.Copy),
                  reads=[("ps%d" % bank, None)], writes=[("u_sb", 2 * mp), ("u_sb", 2 * mp + 1)])
            for m in (2 * mp, 2 * mp + 1):
                a_ = acc[m % 2]
                fw.op("vector", lambda e, m=m, a_=a_: e.tensor_scalar(out=a_[:], in0=u_sb[:, m, 0:NT], scalar1=pcw(m, 0), scalar2=None, op0=ALU.mult),
                      reads=[("u_sb", m), ("pp", None)], writes=[("acc", m % 2)])
                for j in range(1, 4):
                    fw.op("vector", lambda e, m=m, a_=a_, j=j: e.scalar_tensor_tensor(out=a_[:], in0=u_sb[:, m, j:NT + j], scalar=pcw(m, j), in1=a_[:], op0=ALU.mult, op1=ALU.add),
                          reads=[("u_sb", m), ("pp", None), ("acc", m % 2)], writes=[("acc", m % 2)])
                fw.op("scalar", lambda e, m=m, a_=a_: e.activation(out=uaT[:, m, :], in_=a_[:], func=AF.Silu, bias=k.pp[:, l, CB + m:CB + m + 1], scale=1.0),
                      reads=[("acc", m % 2), ("pp", None)], writes=[("uaT" + P, m)])
                yield
        fw.op("gpsimd", lambda e: e.tensor_copy(out=u_sb[:, :, 0:3], in_=u_sb[:, :, NT:NT + 3]), reads=[("u_sb", None)], writes=[("u_sb", None)])
        for mp in range(4):
            bank = mp % 2
            mm_multi(k, [(ps[bank][:, i * NT:(i + 1) * NT], [(Wom[:, kc, (2 * mp + i) * 128:(2 * mp + i + 1) * 128], hT[:, kc, :]) for kc in range(8)]) for i in range(2)],
                     reads=[("Wom", None), ("hT", None)], writes=[("ps%d" % bank, None)])
            fw.op("scalar", lambda e, mp=mp, bank=bank: e.activation(out=sigo[:, 2 * mp:2 * mp + 2, :], in_=ps[bank][:].rearrange("p (i t) -> p i t", t=NT), func=AF.Sigmoid),
                  reads=[("ps%d" % bank, None)], writes=[("sigo" + P, 2 * mp), ("sigo" + P, 2 * mp + 1)])
            yield
            yield
        fw.group("tensor", [lambda e, tt=tt: e.transpose(out=ps[0][:, tt * 4:tt * 4 + 4], in_=g1[0:4, tt * 128:(tt + 1) * 128], identity=cst[0:4, IDENT:IDENT + 4]) for tt in range(2)],
                 reads=[("g1", None), ("consts", None)], writes=[("ps0", None)])
        fw.op("vector", lambda e: e.tensor_copy(out=kscT[:], in_=ps[0][:, 0:8].rearrange("p (t h) -> p t h", h=4)), reads=[("ps0", None)], writes=[("kscT" + P, None)])
        mm_multi(k, [(ps[0][:, 16 + h * NCH:16 + (h + 1) * NCH], [(cst[0:4, SEL + h * 128:SEL + (h + 1) * 128], dect[0:4, 0:NCH])]) for h in range(4)],
                 reads=[("dect", None), ("consts", None)], writes=[("ps0", None)])
        fw.op("vector", lambda e: e.tensor_copy(out=dec_rep[:, :, 0:NCH], in_=ps[0][:, 16:16 + 4 * NCH].rearrange("p (h c) -> p h c", c=NCH)), reads=[("ps0", None)], writes=[("dec_rep" + P, None)])
        for hp2 in range(2):
            bank = hp2
            mm_multi(k, [(ps[bank][:, hh * NT:(hh + 1) * NT], [(cst[0:4, SEL + (2 * hp2 + hh) * 128:SEL + (2 * hp2 + hh + 1) * 128], g2[0:4, :])]) for hh in range(2)],
                     reads=[("g2", None), ("consts", None)], writes=[("ps%d" % bank, None)])
            fw.op("scalar", lambda e, hp2=hp2, bank=bank: e.activation(out=clamp_rep[:, 2 * hp2:2 * hp2 + 2, :], in_=ps[bank][:].rearrange("p (a t) -> p a t", t=NT), func=AF.Copy),
                  reads=[("ps%d" % bank, None)], writes=[("clamp_rep" + P, None)])
        yield
        for tt in range(2):
            for half in range(2):
                bank = half
                mm(k, ps[bank][:], ("ps%d" % bank, None), [(hT[:, kc, tt * 128:(tt + 1) * 128], Wvm[:, kc, half * 512:(half + 1) * 512]) for kc in range(8)], reads=[("Wvm", None), ("hT", None)])
                for hh in range(2):
                    h = half * 2 + hh
                    fw.op("vector", lambda e, tt=tt, h=h, hh=hh, bank=bank: e.tensor_scalar(out=vx[:, tt, h, 0:256], in0=ps[bank][:, hh * 256:(hh + 1) * 256], scalar1=kscT[:, tt, h:h + 1], scalar2=None, op0=ALU.mult),
                          reads=[("ps%d" % bank, None), ("kscT" + P, None)], writes=[("vx" + P, tt * 4 + h)])
                yield
            fw.op("gpsimd", lambda e, tt=tt: e.tensor_tensor(out=vx[:, tt, :, 256:384], in0=bc(cst[:, ONES:ONES + 128].rearrange("p (o c) -> p o c", o=1), [128, 4, 128]),
                                                         in1=bc(kscT[:, tt, :].rearrange("p (h o) -> p h o", o=1), [128, 4, 128]), op=ALU.mult),
                  reads=[("consts", None), ("kscT" + P, None)], writes=[("vx" + P, tt * 4 + h) for h in range(4)])
        for h in range(4):
            for ec in range(2):
                bank = ec
                mm(k, ps[bank][:, 0:NT], ("ps%d" % bank, None), [(Wq[:, dc, h, ec * 128:(ec + 1) * 128], uaT[:, 2 * h + dc, :]) for dc in range(2)], reads=[("Wq", None), ("uaT" + P, 2 * h), ("uaT" + P, 2 * h + 1)])
                fw.op("scalar", lambda e, h=h, ec=ec, bank=bank: e.activation(out=qT[:, 2 * h + ec, :], in_=ps[bank][:, 0:NT], func=AF.Copy), reads=[("ps%d" % bank, None)], writes=[("qT" + P, 2 * h + ec)])
                mm(k, ps[bank][:, NT:2 * NT], ("ps%d" % bank, None), [(Wk[:, dc, h, ec * 128:(ec + 1) * 128], uaT[:, 2 * h + dc, :]) for dc in range(2)], reads=[("Wk", None), ("uaT" + P, 2 * h), ("uaT" + P, 2 * h + 1)])
                fw.op("vector", lambda e, h=h, ec=ec, bank=bank: e.tensor_scalar(out=kT[:, 2 * h + ec, :], in0=ps[bank][:, NT:2 * NT], scalar1=0.0625, scalar2=None, op0=ALU.mult), reads=[("ps%d" % bank, None)], writes=[("kT" + P, 2 * h + ec)])
            yield
        for tt in range(2):
            for h in range(4):
                bank = h % 2
                mm(k, ps[bank][:, 0:256], ("ps%d" % bank, None), [(uaT[:, 2 * h + dc, tt * 128:(tt + 1) * 128], Wk[:, dc, h, :]) for dc in range(2)], reads=[("Wk", None), ("uaT" + P, 2 * h), ("uaT" + P, 2 * h + 1)])
                if h % 2 == 0:
                    fw.op("scalar", lambda e, tt=tt, h=h, bank=bank: e.activation(out=ktm[:, tt, h, :], in_=ps[bank][:, 0:256], func=AF.Copy, scale=0.0625), reads=[("ps%d" % bank, None)], writes=[("ktm" + P, tt * 4 + h)])
                else:
                    fw.op("vector", lambda e, tt=tt, h=h, bank=bank: e.tensor_scalar(out=ktm[:, tt, h, :], in0=ps[bank][:, 0:256], scalar1=0.0625, scalar2=None, op0=ALU.mult), reads=[("ps%d" % bank, None)], writes=[("ktm" + P, tt * 4 + h)])
            yield

    def back(it):
        par = it % 2
        hb = H[par]
        uaT, sigo, qT, kT, ktm, vx, clamp_rep, dec_rep, kscT = (hb[n] for n in ("uaT", "sigo", "qT", "kT", "ktm", "vx", "clamp_rep", "dec_rep", "kscT"))
        P = "%d" % par
        tok0 = it * NT
        seq_start = (tok0 % S) == 0
        seq_last_tile = ((tok0 + NT) % S) == 0
        if seq_start:
            fw.op("gpsimd", lambda e: e.memset(Cx[:], 0.0), writes=[("Cx", None)])
        for cl in range(NCH):
            cols = slice(cl * CH, (cl + 1) * CH)
            SM = SMs[cl]
            mm_multi(k, [(ps[3][:, h * CH:(h + 1) * CH], [(kT[:, 2 * h + ec, cols], qT[:, 2 * h + ec, cols]) for ec in range(2)]) for h in range(4)],
                     reads=[("kT" + P, None), ("qT" + P, None)], writes=[("ps3", None)])
            fw.op("vector", lambda e, SM=SM: e.tensor_tensor(out=SM[:], in0=ps[3][:].rearrange("p (h t) -> p h t", t=CH),
                                                        in1=bc(cst[:, MASK2:MASK2 + CH].rearrange("p (o t) -> p o t", o=1), [128, 4, CH]), op=ALU.mult),
                  reads=[("ps3", None), ("consts", None)], writes=[("SM%d" % cl, None)])
            yield
        for cl in range(NCH):
            tt = cl
            cols = slice(cl * CH, (cl + 1) * CH)
            last_chunk = seq_last_tile and cl == NCH - 1
            SM = SMs[cl]
            def emit_out(h):
                ob = ps[4 + h % 2]
                okey = "ps%d" % (4 + h % 2)
                groups = []
                for j in range(3):
                    prs = [(Cx[:, h, dkc, j * 128:(j + 1) * 128], qT[:, 2 * h + dkc, cols]) for dkc in range(2)]
                    prs.append((vx[:, tt, h, j * 128:(j + 1) * 128], SM[:, h, :]))
                    groups.append((ob[:, j * CH:(j + 1) * CH], prs))
                mm_multi(k, groups, reads=[("Cx", h), ("qT" + P, 2 * h), ("qT" + P, 2 * h + 1), ("vx" + P, tt * 4 + h), ("SM%d" % cl, None)], writes=[(okey, None)])
                ob3 = ob[:, 0:3 * CH].rearrange("p (j t) -> p j t", t=CH)
                fw.op("scalar", lambda e, ob3=ob3: e.activation(out=den[:], in_=ob3[:, 2, :], func=AF.Abs), reads=[(okey, None)], writes=[("den", None)])
                fw.op("vector", lambda e, h=h, cols=cols: e.tensor_tensor(out=den[:], in0=den[:], in1=clamp_rep[:, h, cols], op=ALU.max),
                      reads=[("den", None), ("clamp_rep" + P, None)], writes=[("den", None)])
                fw.op("vector", lambda e: e.reciprocal(out=rec[:], in_=den[:]), reads=[("den", None)], writes=[("rec", None)])
                fw.op("vector", lambda e, ob3=ob3, h=h, cols=cols: e.tensor_tensor(out=hraw[:, h, :, cols], in0=ob3[:, 0:2, :], in1=bc(rec[:].rearrange("p (o t) -> p o t", o=1), [128, 2, CH]), op=ALU.mult),
                      reads=[(okey, None), ("rec", None)], writes=[("hraw", None)])

            def emit_U(h):
                ub = (6, 7) if h % 2 == 0 else (2, 3)
                for dkc in range(2):
                    U = ps[ub[dkc]]
                    mm(k, U[:, 0:384], ("ps%d" % ub[dkc], None),
                       [(k.ident_bf[:], Cx[:, h, dkc, :]), (ktm[:, tt, h, dkc * 128:(dkc + 1) * 128], vx[:, tt, h, :])],
                       reads=[("ident_bf", None), ("Cx", h), ("ktm" + P, tt * 4 + h), ("vx" + P, tt * 4 + h)])
                for dkc in range(2):
                    U = ps[ub[dkc]]
                    dsc = dec_rep[:, h, cl:cl + 1]
                    if dkc == 0:
                        fw.op("scalar", lambda e, U=U, h=h, dkc=dkc, dsc=dsc: e.activation(out=Cx[:, h, dkc, :], in_=U[:, 0:384], func=AF.Identity, scale=dsc, bias=0.0),
                              reads=[("ps%d" % ub[dkc], None), ("dec_rep" + P, None), ("Cx", h)], writes=[("Cx", h)])
                    else:
                        fw.op("vector", lambda e, U=U, h=h, dkc=dkc, dsc=dsc: e.tensor_scalar(out=Cx[:, h, dkc, :], in0=U[:, 0:384], scalar1=dsc, scalar2=None, op0=ALU.mult),
                              reads=[("ps%d" % ub[dkc], None), ("dec_rep" + P, None), ("Cx", h)], writes=[("Cx", h)])

            order = [("o", 0), ("o", 1), ("u", 0), ("o", 2), ("u", 1), ("o", 3), ("u", 2), ("u", 3)]
            for kind, h in order:
                if kind == "o":
                    emit_out(h)
                    yield
                elif not last_chunk:
                    emit_U(h)
                    yield
        for h in range(4):
            sq_ = sqh[h % 2]
            fw.op("scalar", lambda e, h=h, sq_=sq_: e.activation(out=sq_[:], in_=hraw[:, h], func=AF.Square), reads=[("hraw", None)], writes=[("sqh", h % 2)])
            mm(k, ps[3][:, 256:512], ("ps3", None), [(k.ones_bf[:], sq_[:, j, :]) for j in range(2)], reads=[("sqh", h % 2), ("ones_bf", None)])
            fw.op("scalar", lambda e: e.activation(out=lnh[:], in_=ps[3][:, 256:512], func=AF.Ln, scale=1.0 / 256, bias=EPS), reads=[("ps3", None)], writes=[("lnh", None)])
            rs_ = rsh[h % 2]
            fw.op("scalar", lambda e, rs_=rs_: e.activation(out=rs_[:], in_=lnh[:], func=AF.Exp, scale=-0.5), reads=[("lnh", None)], writes=[("rsh", h % 2)])
            for j in range(2):
                m = 2 * h + j
                a_, b_, c_ = tA[m % 2], tB[m % 2], tC[m % 2]
                fw.op("vector", lambda e, h=h, j=j, a_=a_, rs_=rs_: e.tensor_tensor(out=a_[:], in0=hraw[:, h, j, :], in1=rs_[:], op=ALU.mult),
                      reads=[("hraw", None), ("rsh", h % 2)], writes=[("tA", m % 2)])
                fw.op("vector", lambda e, m=m, a_=a_, b_=b_: e.scalar_tensor_tensor(out=b_[:], in0=a_[:], scalar=k.pp[:, l, ONORM + m:ONORM + m + 1], in1=sigo[:, m, :], op0=ALU.mult, op1=ALU.mult),
                      reads=[("tA", m % 2), ("pp", None), ("sigo" + P, m)], writes=[("tB", m % 2)])
                fw.op("vector", lambda e, m=m, c_=c_: e.scalar_tensor_tensor(out=c_[:], in0=uaT[:, m, :], scalar=k.pp[:, l, SKIP + m:SKIP + m + 1], in1=sigo[:, m, :], op0=ALU.mult, op1=ALU.mult),
                      reads=[("uaT" + P, m), ("pp", None), ("sigo" + P, m)], writes=[("tC", m % 2)])
                fw.op("gpsimd", lambda e, m=m, b_=b_, c_=c_: e.tensor_tensor(out=yaT[:, m, :], in0=b_[:], in1=c_[:], op=ALU.add),
                      reads=[("tB", m % 2), ("tC", m % 2)], writes=[("yaT", m)])
            yield
        fw.dma("sync", k.d["yaT"][:, :, tok0:tok0 + NT], yaT[:], reads=[("yaT", None)], writes=[("yaTd", None)])

    run_pipelined(k, T // NT, front, back)
    fw.barrier()
    al.release(mark)


def phase_mixB(k, l):
    fw, nc, al, ps = k.fw, k.nc, k.al, k.ps
    mark = al.mark()
    cst = load_consts(k)
    NT = 512
    NS = NT // 128
    mixw = k.d["mixw"][l]
    Wqa = al.sb("Wqa", [128, 8, 1024], BF16)
    Wka = al.sb("Wka", [128, 8, 512], BF16)
    Wva = al.sb("Wva", [128, 8, 256], BF16)
    load_cast(k, Wqa[:], mixw[:, :, 3080:4104], "Wqa")
    load_cast(k, Wka[:], k.d["wka_dup"][l], "Wka")
    load_cast(k, Wva[:], mixw[:, :, 4360:4616], "Wva")
    xt = al.sb("xt", [128, 8, NT], F32)
    hT = al.sb("hT", [128, 8, NT], BF16)
    lnv = al.sb("lnv", [128, NT], F32)
    rstd = al.sb("rstd", [128, NT], F32)
    hn = [al.sb("hn%d" % i, [128, NT], F32) for i in range(2)]
    cosT = al.sb("cosT", [128, NT], F32)
    sinT = al.sb("sinT", [128, NT], F32)
    ND = 4
    sq2 = [al.sb("sq2_%d" % i, [128, NT], BF16) for i in range(ND)]
    ln2 = [al.sb("ln2_%d" % i, [128, NT], F32) for i in range(2)]
    rs2 = [al.sb("rs2_%d" % i, [128, NT], F32) for i in range(ND)]
    qnw = [al.sb("qnw%d" % i, [128, NT], F32) for i in range(ND)]
    r1 = [al.sb("r1_%d" % i, [128, NT], F32) for i in range(ND)]
    r2 = [al.sb("r2_%d" % i, [128, NT], F32) for i in range(ND)]
    H = []
    for i in range(2):
        H.append(dict(qlo=al.sb("qlo%d" % i, [128, 8, NT], BF16), qhi=al.sb("qhi%d" % i, [128, 8, NT], BF16),
                      krT=al.sb("krT%d" % i, [128, 4, 128 + NT], BF16), vh=al.sb("vh%d" % i, [128, NS + 1, 2, 4, 128], BF16)))
        fw.op("gpsimd", lambda e, i=i: e.memset(H[i]["vh"][:], 0.0), writes=[("vdup%d" % i, None)])
        fw.op("gpsimd", lambda e, i=i: e.memset(H[i]["qlo"][64:128], 0.0), writes=[("qlo%d" % i, None)])
        fw.op("gpsimd", lambda e, i=i: e.memset(H[i]["qhi"][0:64], 0.0), writes=[("qhi%d" % i, None)])
    pT = [al.sb("pT%d" % i, [128, 2, 256], BF16) for i in range(3)]
    ones_tb = al.sb("ones_tb", [128, 2, 128], BF16)
    fw.op("vector", lambda e: e.memset(ones_tb[:], 0.0), writes=[("ones_tb", None)])
    fw.op("vector", lambda e: e.memset(ones_tb[:, 0, 0:64], 1.0), writes=[("ones_tb", None)])
    fw.op("vector", lambda e: e.memset(ones_tb[:, 1, 64:128], 1.0), writes=[("ones_tb", None)])
    es2 = al.sb("es2", [2, 8], F32)
    sinkrow = al.sb("sinkrow", [2, 4, 128], F32)
    fw.op("scalar", lambda e: e.activation(out=es2[:], in_=k.pp[0:2, l, SINK2:SINK2 + 8], func=AF.Exp), reads=[("pp", None)], writes=[("es2", None)])
    fw.op("vector", lambda e: e.tensor_copy(out=sinkrow[:].rearrange("p v (g q) -> p (v g) q", q=64), in_=bc(es2[:].rearrange("p (m o) -> p m o", o=1), [2, 8, 64])),
          reads=[("es2", None)], writes=[("sinkrow", None)])
    ebias = al.sb("ebias", [128, 3], F32)
    fw.op("vector", lambda e: e.memset(ebias[:], 0.0), writes=[("ebias", None)])
    fw.op("vector", lambda e: e.memset(ebias[64:128, 1:2], -10000.0), writes=[("ebias", None)])
    fw.op("vector", lambda e: e.memset(ebias[0:64, 2:3], -10000.0), writes=[("ebias", None)])
    dsum = al.sb("dsum", [128, 2, 64], F32)
    rec = al.sb("recb", [128, 2, 64], F32)
    ybT = al.sb("ybT", [128, 8, NT], BF16)

    def front(it):
        par = it % 2
        P = "%d" % par
        qlo, qhi, krT, vh = (H[par][n] for n in ("qlo", "qhi", "krT", "vh"))
        krT_prev, vh_prev = H[1 - par]["krT"], H[1 - par]["vh"]
        b = (it * NT) // S
        tok0 = it * NT
        seq_start = (tok0 % S) == 0
        fw.dma("sync", hT[:], k.d["hTs"][:, :, tok0:tok0 + NT], writes=[("hT", None)])
        fw.dma("sync", cosT[:], k.d["cosT"][:, tok0:tok0 + NT], writes=[("cosT", None)])
        fw.dma("sync", sinT[:], k.d["sinT"][:, tok0:tok0 + NT], writes=[("sinT", None)])
        if not seq_start:
            fw.op("gpsimd", lambda e: e.tensor_copy(out=krT[:, :, 0:128], in_=krT_prev[:, :, NT:NT + 128]), reads=[("krT%d" % (1 - par), None)], writes=[("krT" + P, None)])
            fw.op("gpsimd", lambda e: e.tensor_copy(out=vh[:, 0], in_=vh_prev[:, NS]), reads=[("vdup%d" % (1 - par), None)], writes=[("vdup" + P, None)])
        yield

        def qk_part1(W_lhs_fn, wkey, wcol, is_q, idx, blk):
            i4 = blk % ND
            bank = 1 + blk % 2
            mm(k, ps[bank][:, 0:NT], ("ps%d" % bank, None), [(W_lhs_fn(kc), hT[:, kc, :]) for kc in range(8)], reads=[(wkey, None), ("hT", None)])
            fw.op("scalar", lambda e: e.activation(out=sq2[i4][:], in_=ps[bank][:, 0:NT], func=AF.Square), reads=[("ps%d" % bank, None)], writes=[("sq2", i4)])
            fw.op("scalar", lambda e: e.activation(out=qnw[i4][:], in_=ps[bank][:, 0:NT], func=AF.Identity, scale=k.pp[:, l, wcol:wcol + 1], bias=0.0),
                  reads=[("ps%d" % bank, None), ("pp", None)], writes=[("qnw", i4)])

        def qk_part2(W_lhs_fn, wkey, wcol, is_q, idx, blk):
            i4 = blk % ND
            i2 = blk % 2
            bank = 1 + i2
            qb = 0
            mm(k, ps[qb][:, 0:NT], ("ps%d" % qb, None), [(k.blk_bf[:], sq2[i4][:])], reads=[("sq2", i4), ("blk_bf", None)])
            mm(k, ps[bank][:, 0:NT], ("ps%d" % bank, None), [(cst[:, ROT:ROT + 128], qnw[i4][:])], reads=[("qnw", i4), ("consts", None)])
            fw.op("scalar", lambda e: e.activation(out=ln2[i2][:], in_=ps[qb][:, 0:NT], func=AF.Ln, scale=1.0 / 64, bias=EPS), reads=[("ps%d" % qb, None)], writes=[("ln2", i2)])
            fw.op("scalar", lambda e: e.activation(out=rs2[i4][:], in_=ln2[i2][:], func=AF.Exp, scale=-0.5), reads=[("ln2", i2)], writes=[("rs2", i4)])
            fw.op("gpsimd", lambda e: e.tensor_tensor(out=r1[i4][:], in0=qnw[i4][:], in1=cosT[:], op=ALU.mult), reads=[("qnw", i4), ("cosT", None)], writes=[("r1", i4)])
            fw.op("vector", lambda e: e.tensor_tensor(out=r2[i4][:], in0=ps[bank][:, 0:NT], in1=sinT[:], op=ALU.mult), reads=[("ps%d" % bank, None), ("sinT", None)], writes=[("r2", i4)])
            fw.op("gpsimd", lambda e: e.tensor_tensor(out=r1[i4][:], in0=r1[i4][:], in1=r2[i4][:], op=ALU.add), reads=[("r1", i4), ("r2", i4)], writes=[("r1", i4)])
            if is_q:
                fw.op("vector", lambda e: e.tensor_tensor(out=qlo[0:64, idx, :], in0=r1[i4][0:64], in1=rs2[i4][0:64], op=ALU.mult), reads=[("r1", i4), ("rs2", i4)], writes=[("qlo" + P, idx)])
                fw.op("gpsimd", lambda e: e.tensor_tensor(out=qhi[64:128, idx, :], in0=r1[i4][64:128], in1=rs2[i4][64:128], op=ALU.mult), reads=[("r1", i4), ("rs2", i4)], writes=[("qhi" + P, idx)])
            else:
                fw.op("vector", lambda e: e.tensor_tensor(out=krT[:, idx, 128:128 + NT], in0=r1[i4][:], in1=rs2[i4][:], op=ALU.mult), reads=[("r1", i4), ("rs2", i4)], writes=[("krT" + P, None)])

        blocks = [(lambda kc, kv=kv: Wka[:, kc, kv * 128:(kv + 1) * 128], "Wka", KW, False, kv) for kv in range(4)]
        blocks += [(lambda kc, m=m: Wqa[:, kc, m * 128:(m + 1) * 128], "Wqa", QW, True, m) for m in range(8)]
        for bi in range(len(blocks)):
            qk_part1(*blocks[bi], bi)
            qk_part2(*blocks[bi], bi)
            yield
            if bi == 3:
                for tt in range(NS):
                    bank = 1 + tt % 2
                    mm(k, ps[bank][:, 0:256], ("ps%d" % bank, None), [(hT[:, kc, tt * 128:(tt + 1) * 128], Wva[:, kc, :]) for kc in range(8)], reads=[("Wva", None), ("hT", None)])
                    for dup in range(2):
                        fw.op("scalar", lambda e, tt=tt, bank=bank, dup=dup: e.activation(out=vh[:, 1 + tt, dup, :, dup * 64:(dup + 1) * 64], in_=ps[bank][:, 0:256].rearrange("p (v d) -> p v d", d=64), func=AF.Copy),
                              reads=[("ps%d" % bank, None)], writes=[("vdup" + P, None)])
                yield

    def back(it):
        par = it % 2
        P = "%d" % par
        qlo, qhi, krT, vdup = (H[par][n] for n in ("qlo", "qhi", "krT", "vh"))
        tok0 = it * NT
        c0 = (tok0 % S) // 64
        steps = []
        for cl in range(NT // 64):
            c = c0 + cl
            subs = {}
            for kc_ in (c - 2, c - 1, c):
                if kc_ < 0:
                    continue
                rel = kc_ - c0
                subs.setdefault((128 + rel * 64) // 128, []).append(rel % 2)
            info = []
            for slot, vs in enumerate(sorted(subs)):
                hv = subs[vs]
                bias = 0 if len(hv) == 2 else (1 if hv[0] == 0 else 2)
                info.append((slot, vs, bias))
            for kv in range(4):
                steps.append((cl, kv, info))

        def emit_S(i):
            cl, kv, info = steps[i]
            qcols = slice(cl * 64, (cl + 1) * 64)
            sb_ = ps[3 + i % 3]
            groups = []
            for (slot, vs, bias) in info:
                kcol = slice(vs * 128, vs * 128 + 128)
                groups.append((sb_[:, slot * 256:slot * 256 + 128], [(krT[:, kv, kcol], qlo[:, 2 * kv:2 * kv + 2, qcols])]))
                groups.append((sb_[:, slot * 256 + 128:slot * 256 + 256], [(krT[:, kv, kcol], qhi[:, 2 * kv:2 * kv + 2, qcols])]))
            mm_multi(k, groups, reads=[("krT" + P, None), ("qlo" + P, 2 * kv), ("qlo" + P, 2 * kv + 1), ("qhi" + P, 2 * kv), ("qhi" + P, 2 * kv + 1)], writes=[("ps%d" % (3 + i % 3), None)])

        def emit_rest(i):
            cl, kv, info = steps[i]
            qcols = slice(cl * 64, (cl + 1) * 64)
            sb_ = ps[3 + i % 3]
            skey = "ps%d" % (3 + i % 3)
            nb_ = ps[6 + i % 2]
            nkey = "ps%d" % (6 + i % 2)
            p_ = pT[i % 3]
            pkey = ("pT", i % 3)
            for (slot, vs, bias) in info:
                fw.op("scalar", lambda e, slot=slot, sb_=sb_, p_=p_, bias=bias: e.activation(out=p_[:, slot, :], in_=sb_[:, slot * 256:(slot + 1) * 256], func=AF.Exp, scale=0.125, bias=ebias[:, bias:bias + 1]),
                      reads=[(skey, None), ("ebias", None)], writes=[pkey])
            prs_n, prs_d = [], []
            for (slot, vs, bias) in info:
                prs_n.append((vdup[:, vs, 0, kv, :], p_[:, slot, 0:128]))
                prs_n.append((vdup[:, vs, 1, kv, :], p_[:, slot, 128:256]))
                prs_d.append((ones_tb[:, 0, :], p_[:, slot, 0:128]))
                prs_d.append((ones_tb[:, 1, :], p_[:, slot, 128:256]))
            prs_d.append((cst[0:2, HSEL:HSEL + 128], sinkrow[0:2, kv, :]))
            mm_multi(k, [(nb_[:, 0:128], prs_n), (nb_[:, 128:256], prs_d)], reads=[("vdup" + P, None), pkey, ("ones_tb", None), ("sinkrow", None), ("consts", None)], writes=[(nkey, None)])
            fw.op("vector", lambda e, nb_=nb_: e.reciprocal(out=rec[:], in_=nb_[:, 128:256].rearrange("p (g q) -> p g q", q=64)), reads=[(nkey, None)], writes=[("recb", None)])
            fw.op("vector", lambda e, nb_=nb_, kv=kv, qcols=qcols: e.tensor_tensor(out=ybT[:, 2 * kv:2 * kv + 2, qcols], in0=nb_[:, 0:128].rearrange("p (g q) -> p g q", q=64), in1=rec[:], op=ALU.mult),
                  reads=[(nkey, None), ("recb", None)], writes=[("ybT", 2 * kv), ("ybT", 2 * kv + 1)])

        emit_S(0)
        emit_S(1)
        for i in range(len(steps)):
            if i + 2 < len(steps):
                emit_S(i + 2)
            emit_rest(i)
            yield
        fw.dma("sync", k.d["ybT"][:, :, tok0:tok0 + NT], ybT[:], reads=[("ybT", None)], writes=[("ybTd", None)])

    run_pipelined(k, T // NT, front, back)
    fw.barrier()
    al.release(mark)


def phase_mixC(k, l):
    fw, nc, al, ps = k.fw, k.nc, k.al, k.ps
    mark = al.mark()
    NT = 512
    Pa = al.sb("Pa", [128, 8, 1024], BF16)
    Pb = al.sb("Pb", [128, 8, 1024], BF16)
    Wm = al.sb("Wm", [128, 8, 2048], BF16)
    Wo = al.sb("Wo", [128, 8, 1024], BF16)
    load_cast(k, Wm[:], k.d["merge_w"][l], "Wm")
    load_cast(k, Pa[:], k.d["proj_a"][l], "Pa")
    load_cast(k, Pb[:], k.d["proj_b"][l], "Pb")
    load_cast(k, Wo[:], k.d["w_out"][l], "Wo")
    lnv = al.sb("lnv", [128, NT], F32)
    rstd = al.sb("rstd", [128, NT], F32)
    hn = [al.sb("hn%d" % i, [128, NT], F32) for i in range(2)]
    H = []
    for i in range(2):
        H.append(dict(xt=al.sb("xt%d" % i, [128, 8, NT], F32), ya=al.sb("ya%d" % i, [128, 8, NT], BF16), yb=al.sb("yb%d" % i, [128, 8, NT], BF16),
                      hT=al.sb("hT%d" % i, [128, 8, NT], BF16)))
    gab = al.sb("gab", [128, 16, NT], BF16)
    m1 = al.sb("m1", [128, NT], F32)
    m2 = al.sb("m2", [128, NT], F32)
    mg = al.sb("mg", [128, 8, NT], BF16)

    def front(it):
        par = it % 2
        P = "%d" % par
        xt, ya, yb, hT = (H[par][n] for n in ("xt", "ya", "yb", "hT"))
        b = (it * NT) // S
        tok0 = it * NT
        fw.dma("sync", xt[:], k.d["xT"][:, :, tok0:tok0 + NT], writes=[("xt" + P, None)])
        fw.dma("sync", ya[:], k.d["yaT"][:, :, tok0:tok0 + NT], writes=[("ya" + P, None)])
        fw.dma("sync", yb[:], k.d["ybT"][:, :, tok0:tok0 + NT], writes=[("yb" + P, None)])
        fw.dma("sync", hT[:], k.d["hTs"][:, :, tok0:tok0 + NT], writes=[("hT" + P, None)])
        yield

    def back(it):
        par = it % 2
        P = "%d" % par
        xt, ya, yb, hT = (H[par][n] for n in ("xt", "ya", "yb", "hT"))
        b = (it * NT) // S
        tok0 = it * NT
        for o in range(16):
            bank = 1 + o % 2
            mm(k, ps[bank][:, 0:NT], ("ps%d" % bank, None), [(Wm[:, kc, o * 128:(o + 1) * 128], hT[:, kc, :]) for kc in range(8)], reads=[("Wm", None), ("hT" + P, None)])
            fw.op("scalar", lambda e, o=o, bank=bank: e.activation(out=gab[:, o, :], in_=ps[bank][:, 0:NT], func=AF.Sigmoid, bias=k.pp[:, l, MB + o:MB + o + 1], scale=1.0),
                  reads=[("ps%d" % bank, None), ("pp", None)], writes=[("gab", o)])
            if o % 4 == 3:
                yield
        for o in range(8):
            ba, bb = (3, 4) if o % 2 == 0 else (7, 0)
            mm(k, ps[ba][:, 0:NT], ("ps%d" % ba, None), [(Pa[:, m, o * 128:(o + 1) * 128], ya[:, m, :]) for m in range(8)], reads=[("Pa", None), ("ya" + P, None)])
            mm(k, ps[bb][:, 0:NT], ("ps%d" % bb, None), [(Pb[:, m, o * 128:(o + 1) * 128], yb[:, m, :]) for m in range(8)], reads=[("Pb", None), ("yb" + P, None)])
            fw.op("vector", lambda e, o=o, ba=ba: e.tensor_tensor(out=m1[:], in0=ps[ba][:, 0:NT], in1=gab[:, o, :], op=ALU.mult), reads=[("ps%d" % ba, None), ("gab", o)], writes=[("m1", None)])
            fw.op("vector", lambda e, o=o, bb=bb: e.tensor_tensor(out=m2[:], in0=ps[bb][:, 0:NT], in1=gab[:, 8 + o, :], op=ALU.mult), reads=[("ps%d" % bb, None), ("gab", 8 + o)], writes=[("m2", None)])
            fw.op("gpsimd", lambda e, o=o: e.tensor_tensor(out=mg[:, o, :], in0=m1[:], in1=m2[:], op=ALU.add), reads=[("m1", None), ("m2", None)], writes=[("mg", o)])
            if o % 2 == 1:
                yield
        for o in range(8):
            bank = 5 + o % 2
            mm(k, ps[bank][:, 0:NT], ("ps%d" % bank, None), [(Wo[:, m, o * 128:(o + 1) * 128], mg[:, m, :]) for m in range(8)], reads=[("Wo", None)] + [("mg", m) for m in range(8)])
            G = k.der[:, l, 5, o, b:b + 1]
            fw.op("vector", lambda e, o=o, bank=bank, G=G: e.scalar_tensor_tensor(out=xt[:, o, :], in0=ps[bank][:, 0:NT], scalar=G, in1=xt[:, o, :], op0=ALU.mult, op1=ALU.add),
                  reads=[("ps%d" % bank, None), ("xt" + P, o), ("der", None)], writes=[("xt" + P, o)])
            if o % 2 == 1:
                yield
        fw.dma("sync", k.d["xT"][:, :, tok0:tok0 + NT], xt[:], reads=[("xt" + P, None)], writes=[("xTd", None)])

    run_pipelined(k, T // NT, front, back)
    fw.barrier()
    al.release(mark)


def build_program(n_layers=L, stop=None, dbg=None, overlap=True):
    nc = bass.Bass("TRN2", target_bir_lowering=False)
    k = K()
    k.nc = nc
    k.fw = FW(nc)
    k.al = Alloc(nc)
    k.overlap = overlap
    d = {}

    def din(name, shape, dt=F32):
        d[name] = nc.dram_tensor(name, shape, dt, kind="ExternalInput").ap()
    din("x", [128, 8, T])
    din("cT", [128, 8, 2])
    din("pos", [NBC, S], I32)
    din("ada_w", [L, 128, 8, 9 * D])
    din("pp", [128, L, NPP])
    din("consts", [128, NCONST])
    din("f1_win", [L, 128, 8 * 2 * FF])
    din("f1_wout", [L, 128, FC * D])
    din("f2_win", [L, 128, 8 * 2 * FF])
    din("f2_wout", [L, 128, FC * D])
    din("mixw", [L, 128, 8, NIN])
    din("wka_dup", [L, 128, 8, 512])
    din("wq", [L, 128, 2, 1024])
    din("wk", [L, 128, 2, 1024])
    din("proj_a", [L, 128, 8, D])
    din("proj_b", [L, 128, 8, D])
    din("merge_w", [L, 128, 8, 2 * D])
    din("w_out", [L, 128, 8, D])
    d["out"] = nc.dram_tensor("out", [128, 8, T], F32, kind="ExternalOutput").ap()
    d["xT"] = nc.dram_tensor("xT_scr", [128, 8, T], F32, kind="Internal").ap()
    d["yaT"] = nc.dram_tensor("yaT_scr", [128, 8, T], BF16, kind="Internal").ap()
    d["ybT"] = nc.dram_tensor("ybT_scr", [128, 8, T], BF16, kind="Internal").ap()
    d["hTs"] = nc.dram_tensor("hT_scr", [128, 8, T], BF16, kind="Internal").ap()
    d["cosT"] = nc.dram_tensor("cosT_scr", [128, T], F32, kind="Internal").ap()
    d["sinT"] = nc.dram_tensor("sinT_scr", [128, T], F32, kind="Internal").ap()
    if dbg is not None:
        d["dbg"] = nc.dram_tensor("dbg", [128, 8, T], F32, kind="ExternalOutput").ap()
    k.d = d
    phase_setup(k)
    done = False
    for l in range(n_layers):
        for stage in ("ffn1", "mixA", "mixB", "mixC", "ffn2"):
            last = (l == n_layers - 1 and stage == "ffn2")
            if stage == "ffn1":
                phase_ffn(k, l, 1, src_tok=False, dst_tok=False, src_ap=(d["x"] if l == 0 else None))
            elif stage == "mixA":
                phase_mixA(k, l)
            elif stage == "mixB":
                phase_mixB(k, l)
            elif stage == "mixC":
                phase_mixC(k, l)
            else:
                phase_ffn(k, l, 2, src_tok=False, dst_tok=False, dst_ap=(d["out"] if last else None))
            if stop is not None and (l, stage) == tuple(stop):
                done = True
                break
        if done:
            break
    if dbg is not None:
        al = k.al
        t = al.sb("dbgt", [128, 8, 512], F32)
        src = d["xT"]
        for i in range(T // 512):
            k.fw.dma("sync", t[:], src[:, :, i * 512:(i + 1) * 512], writes=[("dbgt", None)])
            k.fw.dma("sync", d["dbg"][:, :, i * 512:(i + 1) * 512], t[:], reads=[("dbgt", None)], writes=[("dbgd", None)])
    k.fw.finish()
    return nc


def _pk(w, kc=8):
    K_, N = w.shape
    return np.ascontiguousarray(w.reshape(K_ // 128, 128, N).transpose(1, 0, 2))


def _vec(v):
    return np.ascontiguousarray(v.reshape(-1, 128).T)


def make_consts():
    c = np.zeros((128, NCONST), np.float32)
    c[:, IDENT:IDENT + 128] = np.eye(128, dtype=np.float32)
    c[:, ONES:ONES + 128] = 1.0
    for p in range(128):
        for m in range(128):
            if p // 64 == m // 64:
                c[p, BLK + m] = 1.0
    for m in range(128):
        if (m % 64) < 32:
            c[m + 32, ROT + m] = -1.0
        else:
            c[m - 32, ROT + m] = 1.0
    for p in range(128):
        for t in range(64):
            if (p % 64) <= t:
                c[p, MASK + t] = 1.0
    for p in range(128):
        c[p, MASK2 + p:MASK2 + 128] = 1.0
    c[0, HSEL:HSEL + 64] = 1.0
    c[1, HSEL + 64:HSEL + 128] = 1.0
    for h in range(4):
        c[h, SEL + h * 128:SEL + (h + 1) * 128] = 1.0
    j = (np.arange(128) % 32).astype(np.float32)
    c[:, INVF] = (np.float32(10000.0) ** (-(2.0 * j) / np.float32(64.0))).astype(np.float32)
    return c


def prep_shared(inp):
    f = lambda a: np.asarray(a, dtype=np.float32)
    sh = {}
    sh["ada_w"] = np.stack([_pk(f(inp["ada_w"][l])) for l in range(L)])
    sh["f1_win"] = np.stack([_pk(f(inp["ffn1_w_in"][l])).reshape(128, -1) for l in range(L)])
    sh["f1_wout"] = np.stack([_pk(f(inp["ffn1_w_out"][l])).reshape(128, -1) for l in range(L)])
    sh["f2_win"] = np.stack([_pk(f(inp["ffn2_w_in"][l])).reshape(128, -1) for l in range(L)])
    sh["f2_wout"] = np.stack([_pk(f(inp["ffn2_w_out"][l])).reshape(128, -1) for l in range(L)])
    sh["mixw"] = np.stack([_pk(f(inp["mix_w_in"][l])) for l in range(L)])
    wka = []
    for l in range(L):
        ka = _pk(f(inp["mix_w_in"][l][:, 4104:4360]))
        ka = ka.reshape(128, 8, 4, 1, 64)
        wka.append(np.ascontiguousarray(np.broadcast_to(ka, (128, 8, 4, 2, 64)).reshape(128, 8, 512)))
    sh["wka_dup"] = np.stack(wka)
    def hw(w):
        w = f(w).reshape(4, 2, 128, 256)
        return np.ascontiguousarray(w.transpose(2, 1, 0, 3).reshape(128, 2, 1024))
    sh["wq"] = np.stack([hw(inp["m_wq"][l]) for l in range(L)])
    sh["wk"] = np.stack([hw(inp["m_wk"][l]) for l in range(L)])
    sh["proj_a"] = np.stack([_pk(f(inp["proj_a"][l])) for l in range(L)])
    sh["proj_b"] = np.stack([_pk(f(inp["proj_b"][l])) for l in range(L)])
    sh["merge_w"] = np.stack([_pk(f(inp["merge_w"][l])) for l in range(L)])
    sh["w_out"] = np.stack([_pk(f(inp["w_out"][l])) for l in range(L)])
    pp = np.zeros((128, L, NPP), np.float32)
    for l in range(L):
        pp[:, l, ADAB:ADAB + 72] = _vec(f(inp["ada_b"][l]))
        pp[:, l, N1:N1 + 8] = _vec(f(inp["ffn1_norm"][l]))
        pp[:, l, N2:N2 + 8] = _vec(f(inp["mix_norm"][l]))
        pp[:, l, N3:N3 + 8] = _vec(f(inp["ffn2_norm"][l]))
        cw = f(inp["m_conv_w"][l])
        for j in range(4):
            pp[:, l, CW + j:CW + 32:4] = _vec(cw[j])
        pp[:, l, CB:CB + 8] = _vec(f(inp["m_conv_b"][l]))
        pp[:, l, ONORM:ONORM + 8] = _vec(f(inp["m_out_norm"][l]))
        pp[:, l, SKIP:SKIP + 8] = _vec(f(inp["m_skip"][l]))
        pp[:, l, MB:MB + 16] = _vec(f(inp["merge_b"][l]))
        pp[:, l, QW] = np.concatenate([f(inp["a_q_norm"][l])] * 2)
        pp[:, l, KW] = np.concatenate([f(inp["a_k_norm"][l])] * 2)
        sk = f(inp["a_sinks"][l])
        for m in range(8):
            pp[0:64, l, SINK + m] = sk[2 * m]
            pp[64:128, l, SINK + m] = sk[2 * m + 1]
        for m in range(8):
            pp[0, l, SINK2 + m] = sk[2 * m]
            pp[1, l, SINK2 + m] = sk[2 * m + 1]
        gb = f(inp["m_gate_b"][l])
        pp[0:4, l, GBI] = gb[0:4]
        pp[0:4, l, GBF] = gb[4:8]
    sh["pp"] = pp
    sh["consts"] = make_consts()
    return sh


def make_in_maps(inp):
    sh = prep_shared(inp)
    x = np.asarray(inp["x"], dtype=np.float32)
    c = np.asarray(inp["c"], dtype=np.float32)
    pos = np.asarray(inp["positions"], dtype=np.int32)
    maps = []
    for core in range(NCORES):
        bs = slice(core * NBC, (core + 1) * NBC)
        m = dict(sh)
        m["x"] = np.ascontiguousarray(x[bs].reshape(T, 8, 128).transpose(2, 1, 0))
        m["cT"] = np.ascontiguousarray(c[bs].reshape(NBC, 8, 128).transpose(2, 1, 0))
        m["pos"] = np.ascontiguousarray(pos[bs])
        maps.append(m)
    return maps


def kernel(**inputs):
    maps = make_in_maps(inputs)
    nc = build_program()
    res = run_bass_kernel_spmd(nc, maps, core_ids=list(range(NCORES)))
    out = np.stack([np.ascontiguousarray(np.asarray(r["out"]).transpose(2, 1, 0)).reshape(NBC, S, D) for r in res.results])
    return out.reshape(NCORES * NBC, S, D).astype(np.float32)
```

```python
import numpy as np
import concourse.bass as bass
import concourse.mybir as mybir
from concourse.bass_utils import run_bass_kernel_spmd

F32 = mybir.dt.float32
BF16 = mybir.dt.bfloat16
I32 = mybir.dt.int32
ALU = mybir.AluOpType
AF = mybir.ActivationFunctionType

NCORES = 8
L = 2
D = 1024
S = 2048
NBC = 2
T = NBC * S
FF = 2816
FC = 22
NIN = 4616
EPS = 1e-6
TWO_PI = 6.283185307179586
C1 = 6.28125
C2 = TWO_PI - C1

ADAB, N1, N2, N3, CW, CB, ONORM, SKIP, MB, QW, KW, SINK, GBI, GBF, SINK2, NPP = 0, 72, 80, 88, 96, 128, 136, 144, 152, 168, 169, 170, 178, 179, 180, 188
IDENT, ONES, BLK, ROT, MASK, SEL, INVF, MASK2, HSEL, NCONST = 0, 128, 256, 384, 512, 576, 1088, 1090, 1218, 1346

ENGS = ("sync", "scalar", "vector", "gpsimd", "tensor")


class FW:
    def __init__(self, nc, n_dma_sems=40, n_sw_sems=54):
        self.nc = nc
        self.prog = {e: [] for e in ENGS}
        self.cnt = {e: 0 for e in ENGS}
        self.waited = {e: {} for e in ENGS}
        self.state = {}
        self.n_dma = n_dma_sems
        self.dry = False
        self.dma_cnt = [0] * n_dma_sems
        self.dma_rr = 0
        self.sems = {}
        self._cm = []
        for e in ENGS:
            cm = nc.semaphore("s_" + e)
            self.sems[e] = cm.__enter__()
            self._cm.append(cm)
        for i in range(n_dma_sems):
            cm = nc.semaphore("d_%d" % i)
            self.sems["dma%d" % i] = cm.__enter__()
            self._cm.append(cm)
        self.n_sw = n_sw_sems
        self.sw_used = 0
        for i in range(n_sw_sems):
            cm = nc.semaphore("w_%d" % i)
            self.sems["sw%d" % i] = cm.__enter__()
            self._cm.append(cm)

    def _st(self, key):
        name, idx = key
        d = self.state.setdefault(name, {})
        return d.setdefault(idx, [None, {}])

    def _deps(self, key, write):
        name, idx = key
        d = self.state.setdefault(name, {})
        out = []
        ks = list(d.keys()) if idx is None else [k for k in (idx, None) if k in d]
        for k in ks:
            st = d[k]
            if st[0] is not None:
                out.append(st[0])
            if write:
                out.extend(st[1].items())
        return out

    def _emit_waits(self, eng, deps):
        need = {}
        for (s, v) in deps:
            if eng == "tensor" and s == "tensor":
                continue
            if need.get(s, 0) < v:
                need[s] = v
        for s, v in need.items():
            if self.waited[eng].get(s, 0) >= v:
                continue
            self.waited[eng][s] = v
            sem = self.sems[s]
            self.prog[eng].append(lambda e, sem=sem, v=v: e.wait_ge(sem, v))

    def _register(self, ev, reads, writes):
        s, v = ev
        for k in reads:
            r = self._st(k)[1]
            if r.get(s, 0) < v:
                r[s] = v
        for k in writes:
            name, idx = k
            if idx is None:
                self.state[name] = {None: [ev, {}]}
            else:
                st = self._st(k)
                st[0] = ev
                st[1] = {}

    def _alldeps(self, reads, writes):
        deps = []
        for k in reads:
            deps += self._deps(k, k[0].startswith("ps"))
        for k in writes:
            deps += self._deps(k, True)
        return deps

    def group(self, eng, fns, reads=(), writes=()):
        if self.dry:
            return
        self._emit_waits(eng, self._alldeps(reads, writes))
        for fn in fns[:-1]:
            self.prog[eng].append(lambda e, fn=fn: fn(e))
        self.cnt[eng] += 1
        v = self.cnt[eng]
        sem = self.sems[eng]
        fn = fns[-1]
        self.prog[eng].append(lambda e, fn=fn, sem=sem: fn(e).then_inc(sem, 1))
        self._register((eng, v), reads, writes)

    def op(self, eng, fn, reads=(), writes=()):
        self.group(eng, [fn], reads, writes)

    def dma(self, eng, out, in_, reads=(), writes=(), **kw):
        if self.dry:
            return
        deps = self._alldeps(reads, writes)
        if eng == "gpsimd":
            assert self.sw_used < self.n_sw, "out of software-DMA semaphores"
            sname = "sw%d" % self.sw_used
            self.sw_used += 1
            self._emit_waits(eng, deps)
            v = 16
        else:
            i = self.dma_rr
            self.dma_rr = (self.dma_rr + 1) % self.n_dma
            sname = "dma%d" % i
            if self.dma_cnt[i] > 0:
                deps.append((sname, self.dma_cnt[i]))
            self._emit_waits(eng, deps)
            self.dma_cnt[i] += 16
            v = self.dma_cnt[i]
        sem = self.sems[sname]
        self.prog[eng].append(
            lambda e, out=out, in_=in_, sem=sem, kw=kw: e.dma_start(out=out, in_=in_, **kw).then_inc(sem, 16))
        self._register((sname, v), reads, writes)

    def barrier(self):
        if self.dry:
            return
        for e in ENGS:
            deps = [(o, self.cnt[o]) for o in ENGS if o != e and self.cnt[o] > 0]
            deps += [("dma%d" % i, self.dma_cnt[i]) for i in range(self.n_dma) if self.dma_cnt[i] > 0]
            deps += [("sw%d" % i, 16) for i in range(self.sw_used)]
            self._emit_waits(e, deps)

    def finish(self):
        self.barrier()
        nc = self.nc
        with nc.Block() as block:
            for en in ENGS:
                prog = self.prog[en]

                def body(e, prog=prog):
                    for f in prog:
                        f(e)
                getattr(block, en)(body)
        for cm in reversed(self._cm):
            cm.__exit__(None, None, None)


class K:
    pass


def mm(k, out, okey, pairs, reads):
    n = len(pairs)
    fns = []
    for i, (l, r) in enumerate(pairs):
        fns.append(lambda e, l=l, r=r, i=i: e.matmul(out, lhsT=l, rhs=r, start=(i == 0), stop=(i == n - 1)))
    k.fw.group("tensor", fns, reads=reads, writes=[okey])


def mm_multi(k, groups, reads, writes):
    fns = []
    for (out, pairs) in groups:
        n = len(pairs)
        for i, (l, r) in enumerate(pairs):
            fns.append(lambda e, out=out, l=l, r=r, i=i, n=n: e.matmul(out, lhsT=l, rhs=r, start=(i == 0), stop=(i == n - 1)))
    k.fw.group("tensor", fns, reads=reads, writes=writes)


def bc(ap, shape):
    return ap.broadcast_to(shape)


class Alloc:
    def __init__(self, nc):
        self.nc = nc
        self.stack = []
        self.n = 0

    def sb(self, name, shape, dt):
        self.n += 1
        cm = self.nc.sbuf_tensor("sb%d_%s" % (self.n, name), shape, dt)
        t = cm.__enter__()
        self.stack.append(cm)
        return t

    def ps(self, name, shape, dt):
        cm = self.nc.psum_tensor(name, shape, dt)
        t = cm.__enter__()
        self.stack.append(cm)
        return t

    def mark(self):
        return len(self.stack)

    def release(self, mark):
        while len(self.stack) > mark:
            self.stack.pop().__exit__(None, None, None)


def load_cast(k, dst3, src3, key):
    k.fw.dma("gpsimd", dst3, src3, writes=[(key, None)])


def rms_h(k, xt, ntok, b, l, sub, hT, scratch_bf, lnv, rstd, hn, ps_bank, ps_key, xkey="xt", hkey="hT", split_sq=False):
    fw = k.fw
    if split_sq:
        for kc in range(8):
            fw.op("scalar", lambda e, kc=kc: e.activation(out=scratch_bf[0][:, kc, :], in_=xt[:, kc, :], func=AF.Square),
                  reads=[(xkey, kc)], writes=[(scratch_bf[1], kc)])
    else:
        fw.op("scalar", lambda e: e.activation(out=scratch_bf[0][:, 0:8, :], in_=xt[:], func=AF.Square),
              reads=[(xkey, None)], writes=[(scratch_bf[1], i) for i in range(8)])
    if split_sq:
        for kc in range(8):
            fw.op("tensor", lambda e, kc=kc: e.matmul(ps_bank[:, 0:ntok], lhsT=k.ones_bf[:], rhs=scratch_bf[0][:, kc, :], start=(kc == 0), stop=(kc == 7)),
                  reads=[(scratch_bf[1], kc), ("ones_bf", None)], writes=[ps_key])
    else:
        mm(k, ps_bank[:, 0:ntok], ps_key, [(k.ones_bf[:], scratch_bf[0][:, kc, :]) for kc in range(8)],
           reads=[(scratch_bf[1], i) for i in range(8)])
    rk = "lnv" if rstd is lnv else "rstd"
    fw.op("scalar", lambda e: e.activation(out=lnv[:], in_=ps_bank[:, 0:ntok], func=AF.Ln, scale=1.0 / D, bias=EPS),
          reads=[ps_key], writes=[("lnv", None)])
    fw.op("scalar", lambda e: e.activation(out=rstd[:], in_=lnv[:], func=AF.Exp, scale=-0.5),
          reads=[("lnv", None)], writes=[(rk, None)])
    for kc in range(8):
        h_ = hn[kc % 2]
        fw.op("gpsimd" if kc in (1, 4, 6) else "vector", lambda e, kc=kc, h_=h_: e.tensor_tensor(out=h_[:], in0=xt[:, kc, :], in1=rstd[:], op=ALU.mult),
              reads=[(xkey, kc), (rk, None)], writes=[("hn", kc % 2)])
        A = k.der[:, l, 3 * sub + 0, kc, b:b + 1]
        sh = k.der[:, l, 3 * sub + 1, kc, b:b + 1]
        fw.op("scalar", lambda e, kc=kc, h_=h_, A=A, sh=sh: e.activation(out=hT[:, kc, :], in_=h_[:], func=AF.Identity, scale=A, bias=sh),
              reads=[("hn", kc % 2), ("der", None)], writes=[(hkey, kc)])


def load_consts(k):
    cst = k.al.sb("consts", [128, NCONST], F32)
    k.fw.dma("sync", cst[:], k.d["consts"], writes=[("consts", None)])
    return cst


def phase_setup(k):
    fw, nc, al = k.fw, k.nc, k.al
    k.pp = al.sb("pp", [128, L, NPP], F32)
    k.der = al.sb("der", [128, L, 9, 8, 2], F32)
    k.ident_bf = al.sb("ident_bf", [128, 128], BF16)
    k.ones_bf = al.sb("ones_bf", [128, 128], BF16)
    k.blk_bf = al.sb("blk_bf", [128, 128], BF16)
    k.nbf = al.sb("nbf", [4, L], F32)
    k.esink = al.sb("esink", [128, L, 8], F32)
    k.ps = [al.ps("psb%d" % i, [128, 512], F32) for i in range(8)]
    fw.dma("sync", k.pp[:], k.d["pp"], writes=[("pp", None)])
    mark = al.mark()
    cst = load_consts(k)
    invf = cst[:, INVF:INVF + 1]
    fw.op("vector", lambda e: e.tensor_copy(out=k.ident_bf[:], in_=cst[:, IDENT:IDENT + 128]), reads=[("consts", None)], writes=[("ident_bf", None)])
    fw.op("vector", lambda e: e.tensor_copy(out=k.ones_bf[:], in_=cst[:, ONES:ONES + 128]), reads=[("consts", None)], writes=[("ones_bf", None)])
    fw.op("vector", lambda e: e.tensor_copy(out=k.blk_bf[:], in_=cst[:, BLK:BLK + 128]), reads=[("consts", None)], writes=[("blk_bf", None)])
    for l in range(L):
        fw.op("vector", lambda e, l=l: e.tensor_scalar(out=k.nbf[:, l:l + 1], in0=k.pp[0:4, l, GBF:GBF + 1], scalar1=-1.0, scalar2=None, op0=ALU.mult),
              reads=[("pp", None)], writes=[("nbf", None)])
        fw.op("scalar", lambda e, l=l: e.activation(out=k.esink[:, l, :], in_=k.pp[:, l, SINK:SINK + 8], func=AF.Exp),
              reads=[("pp", None)], writes=[("esink", None)])
    cT = al.sb("cT", [128, 8, 2], F32)
    cact = al.sb("cact", [128, 8, 2], F32)
    stage = al.sb("adaw_st", [128, 2, 8, 512], F32)
    modT = al.sb("modT", [128, 72, 2], F32)
    fw.dma("sync", cT[:], k.d["cT"], writes=[("cT", None)])
    fw.op("scalar", lambda e: e.activation(out=cact[:], in_=cT[:], func=AF.Silu), reads=[("cT", None)], writes=[("cact", None)])
    ps7 = k.ps[7]
    for l in range(L):
        for g in range(18):
            fw.dma("sync", stage[:, g % 2], k.d["ada_w"][l, :, :, g * 512:(g + 1) * 512], writes=[("adaw_st", g % 2)])
            for jj in range(4):
                j = g * 4 + jj
                mm(k, ps7[:, 2 * j:2 * j + 2], ("ps7", None),
                   [(stage[:, g % 2, kc, jj * 128:(jj + 1) * 128], cact[:, kc, :]) for kc in range(8)],
                   reads=[("adaw_st", g % 2), ("cact", None)])
        fw.op("vector", lambda e, l=l: e.tensor_tensor(out=modT[:], in0=ps7[:, 0:144].rearrange("p (j b) -> p j b", b=2),
                                                   in1=bc(k.pp[:, l, ADAB:ADAB + 72].rearrange("p (j o) -> p j o", o=1), [128, 72, 2]), op=ALU.add),
              reads=[("ps7", None), ("pp", None)], writes=[("modT", None)])
        for sub in range(3):
            ncol = (N1, N2, N3)[sub]
            sh = modT[:, (3 * sub) * 8:(3 * sub) * 8 + 8, :]
            sc = modT[:, (3 * sub + 1) * 8:(3 * sub + 1) * 8 + 8, :]
            g_ = modT[:, (3 * sub + 2) * 8:(3 * sub + 2) * 8 + 8, :]
            nrm = bc(k.pp[:, l, ncol:ncol + 8].rearrange("p (j o) -> p j o", o=1), [128, 8, 2])
            fw.op("vector", lambda e, l=l, sub=sub, sc=sc, nrm=nrm: e.scalar_tensor_tensor(out=k.der[:, l, 3 * sub + 0], in0=sc, scalar=1.0, in1=nrm, op0=ALU.add, op1=ALU.mult),
                  reads=[("modT", None), ("pp", None)], writes=[("der", None)])
            fw.op("vector", lambda e, l=l, sub=sub, sh=sh: e.tensor_copy(out=k.der[:, l, 3 * sub + 1], in_=sh),
                  reads=[("modT", None)], writes=[("der", None)])
            gs = 1.0 if sub == 1 else 0.5
            fw.op("vector", lambda e, l=l, sub=sub, g_=g_, gs=gs: e.tensor_scalar(out=k.der[:, l, 3 * sub + 2], in0=g_, scalar1=gs, scalar2=None, op0=ALU.mult),
                  reads=[("modT", None)], writes=[("der", None)])
    posi = al.sb("posi", [128, S], I32)
    ang = al.sb("ang", [128, S], F32)
    t0 = al.sb("rt0", [128, S], F32)
    ni = al.sb("rni", [128, S], I32)
    nf = al.sb("rnf", [128, S], F32)
    sn = al.sb("rsn", [128, S], F32)
    cs = al.sb("rcs", [128, S], F32)
    for b in range(NBC):
        fw.dma("sync", posi[:], k.d["pos"][b:b + 1, :].broadcast_to([128, S]), writes=[("posi", None)])
        fw.op("vector", lambda e: e.tensor_copy(out=ang[:], in_=posi[:]), reads=[("posi", None)], writes=[("ang", None)])
        fw.op("vector", lambda e: e.tensor_scalar(out=ang[:], in0=ang[:], scalar1=invf, scalar2=None, op0=ALU.mult),
              reads=[("ang", None), ("consts", None)], writes=[("ang", None)])
        fw.op("vector", lambda e: e.tensor_scalar(out=t0[:], in0=ang[:], scalar1=1.0 / TWO_PI, scalar2=None, op0=ALU.mult),
              reads=[("ang", None)], writes=[("rt0", None)])
        fw.op("vector", lambda e: e.tensor_copy(out=ni[:], in_=t0[:]), reads=[("rt0", None)], writes=[("rni", None)])
        fw.op("vector", lambda e: e.tensor_copy(out=nf[:], in_=ni[:]), reads=[("rni", None)], writes=[("rnf", None)])
        fw.op("vector", lambda e: e.scalar_tensor_tensor(out=t0[:], in0=nf[:], scalar=-C1, in1=ang[:], op0=ALU.mult, op1=ALU.add),
              reads=[("rnf", None), ("ang", None)], writes=[("rt0", None)])
        fw.op("vector", lambda e: e.scalar_tensor_tensor(out=t0[:], in0=nf[:], scalar=-C2, in1=t0[:], op0=ALU.mult, op1=ALU.add),
              reads=[("rnf", None), ("rt0", None)], writes=[("rt0", None)])
        fw.op("vector", lambda e: e.tensor_scalar(out=t0[:], in0=t0[:], scalar1=3.1415925, scalar2=-3.1415925, op0=ALU.min, op1=ALU.max),
              reads=[("rt0", None)], writes=[("rt0", None)])
        fw.op("scalar", lambda e: e.activation(out=sn[:], in_=t0[:], func=AF.Sin), reads=[("rt0", None)], writes=[("rsn", None)])
        fw.op("scalar", lambda e: e.activation(out=cs[:], in_=t0[:], func=AF.Sin, scale=0.5), reads=[("rt0", None)], writes=[("rcs", None)])
        fw.op("vector", lambda e: e.tensor_tensor(out=cs[:], in0=cs[:], in1=cs[:], op=ALU.mult), reads=[("rcs", None)], writes=[("rcs", None)])
        fw.op("vector", lambda e: e.tensor_scalar(out=cs[:], in0=cs[:], scalar1=-2.0, scalar2=1.0, op0=ALU.mult, op1=ALU.add),
              reads=[("rcs", None)], writes=[("rcs", None)])
        fw.dma("sync", k.d["sinT"][:, b * S:(b + 1) * S], sn[:], reads=[("rsn", None)], writes=[("sinTd", None)])
        fw.dma("sync", k.d["cosT"][:, b * S:(b + 1) * S], cs[:], reads=[("rcs", None)], writes=[("cosTd", None)])
    fw.barrier()
    al.release(mark)


def phase_ffn(k, l, which, src_tok, dst_tok, src_ap=None, dst_ap=None):
    fw, nc, al, ps = k.fw, k.nc, k.al, k.ps
    sub = 0 if which == 1 else 2
    mark = al.mark()
    NT = 512
    W1 = al.sb("W1", [128, 8, 2 * FF], BF16)
    W2 = al.sb("W2", [128, FC, D], BF16)
    w1d = k.d["f%d_win" % which][l].rearrange("p (a n) -> p a n", n=2 * FF)
    w1d4 = w1d.rearrange("p a (two n) -> p a two n", two=2)
    W14 = W1[:].rearrange("p a (two n) -> p a two n", two=2)
    for gq in range(6):
        c0, c1 = gq * 512, min((gq + 1) * 512, FF)
        fw.dma("gpsimd", W14[:, :, :, c0:c1], w1d4[:, :, :, c0:c1], writes=[("W1a", gq), ("W1g", gq)])
    load_cast(k, W2[:].rearrange("p a n -> p (a n)").rearrange("p (a e) -> p a e", e=2048),
              k.d["f%d_wout" % which][l].rearrange("p (a e) -> p a e", e=2048), "W2")
    xts = [al.sb("xt%d" % i, [128, 8, NT], F32) for i in range(2)]
    hT = al.sb("hT", [128, 8, NT], BF16)
    actT = al.sb("actT", [128, FC, NT], BF16)
    lnv = al.sb("lnv", [128, NT], F32)
    rstd = lnv
    srcx = src_ap if src_ap is not None else k.d["xT"]
    hn = [al.sb("hn%d" % i, [128, NT], F32) for i in range(2)]
    sa = [al.sb("sa%d" % i, [128, NT], F32) for i in range(2)]
    xin = al.sb("xin", [128, 2, D], F32) if (src_tok or dst_tok) else None
    ev = 0
    for it in range(T // NT):
        b = (it * NT) // S
        tok0 = it * NT
        if src_tok:
            for tb in range(4):
                fw.dma("sync", xin[:, tb % 2, :], k.d["x"][tok0 + tb * 128:tok0 + (tb + 1) * 128, :], writes=[("xin", tb % 2)])
                for half in range(2):
                    bank = (tb * 2 + half) % 4
                    fns = [lambda e, q=q, bank=bank, half=half, tb=tb: e.transpose(out=ps[bank][:, q * 128:(q + 1) * 128], in_=xin[:, tb % 2, (half * 4 + q) * 128:(half * 4 + q + 1) * 128], identity=k.ident_f)
                           for q in range(4)]
                    fw.group("tensor", fns, reads=[("xin", tb % 2), ("consts", None)], writes=[("ps%d" % bank, None)])
                    src = ps[bank][:].rearrange("p (q t) -> p q t", t=128)
                    dst = xt[:, half * 4:(half + 1) * 4, tb * 128:(tb + 1) * 128]
                    eng = "vector" if ev % 2 == 0 else "scalar"
                    ev += 1
                    if eng == "vector":
                        fw.op("vector", lambda e, src=src, dst=dst: e.tensor_copy(out=dst, in_=src), reads=[("ps%d" % bank, None)], writes=[("xt", half * 4 + q) for q in range(4)])
                    else:
                        fw.op("scalar", lambda e, src=src, dst=dst: e.activation(out=dst, in_=src, func=AF.Copy), reads=[("ps%d" % bank, None)], writes=[("xt", half * 4 + q) for q in range(4)])
        else:
            xk = "xt%d" % (it % 2)
            xt = xts[it % 2]
            if it == 0:
                for kc in range(8):
                    fw.dma("sync", xt[:, kc, :], srcx[:, kc, tok0:tok0 + NT], writes=[(xk, kc)])
            if it + 1 < T // NT:
                for kc in range(8):
                    fw.dma("sync", xts[(it + 1) % 2][:, kc, :], srcx[:, kc, tok0 + NT:tok0 + 2 * NT], writes=[("xt%d" % ((it + 1) % 2), kc)])
        rms_h(k, xt, NT, b, l, sub, hT, (actT, "actT"), lnv, rstd, hn, ps[6], ("ps6", None), xkey=xk, split_sq=not src_tok)
        for j in range(FC):
            pa_, pg_ = ps[j % 2], ps[2 + j % 2]
            if j == 0:
                for kc in range(8):
                    fw.op("tensor", lambda e, kc=kc, pa_=pa_: e.matmul(pa_[:], lhsT=W1[:, kc, 0:128], rhs=hT[:, kc, :], start=(kc == 0), stop=(kc == 7)),
                          reads=[("W1a", 0), ("hT", kc)], writes=[("ps0", None)])
                    fw.op("tensor", lambda e, kc=kc, pg_=pg_: e.matmul(pg_[:], lhsT=W1[:, kc, FF:FF + 128], rhs=hT[:, kc, :], start=(kc == 0), stop=(kc == 7)),
                          reads=[("W1g", 0), ("hT", kc)], writes=[("ps2", None)])
            else:
                mm(k, pa_[:], ("ps%d" % (j % 2), None), [(W1[:, kc, j * 128:(j + 1) * 128], hT[:, kc, :]) for kc in range(8)], reads=[("W1a", j // 4), ("hT", None)])
                mm(k, pg_[:], ("ps%d" % (2 + j % 2), None), [(W1[:, kc, FF + j * 128:FF + (j + 1) * 128], hT[:, kc, :]) for kc in range(8)], reads=[("W1g", j // 4), ("hT", None)])
            s_ = sa[j % 2]
            fw.op("scalar", lambda e, s_=s_, pa_=pa_: e.activation(out=s_[:], in_=pa_[:], func=AF.Silu), reads=[("ps%d" % (j % 2), None)], writes=[("sa", j % 2)])
            fw.op("vector", lambda e, s_=s_, pg_=pg_, j=j: e.tensor_tensor(out=actT[:, j, :], in0=s_[:], in1=pg_[:], op=ALU.mult),
                  reads=[("sa", j % 2), ("ps%d" % (2 + j % 2), None)], writes=[("actT", j)])
        for o in range(8):
            py = ps[4 + o % 2]
            if o == 0:
                for j in range(FC):
                    fw.op("tensor", lambda e, j=j, py=py: e.matmul(py[:], lhsT=W2[:, j, 0:128], rhs=actT[:, j, :], start=(j == 0), stop=(j == FC - 1)),
                          reads=[("W2", None), ("actT", j)], writes=[("ps4", None)])
            else:
                mm(k, py[:], ("ps%d" % (4 + o % 2), None), [(W2[:, j, o * 128:(o + 1) * 128], actT[:, j, :]) for j in range(FC)],
                   reads=[("W2", None)] + [("actT", j) for j in range(FC)])
            G = k.der[:, l, 3 * sub + 2, o, b:b + 1]
            fw.op("vector", lambda e, py=py, o=o, G=G, xt=xt: e.scalar_tensor_tensor(out=xt[:, o, :], in0=py[:], scalar=G, in1=xt[:, o, :], op0=ALU.mult, op1=ALU.add),
                  reads=[("ps%d" % (4 + o % 2), None), (xk, o), ("der", None)], writes=[(xk, o)])
            if not dst_tok:
                fw.dma("scalar", (dst_ap if dst_ap is not None else k.d["xT"])[:, o, tok0:tok0 + NT], xt[:, o, :], reads=[(xk, o)], writes=[("xTd", o)])
        if dst_tok:
            for tb in range(4):
                for half in range(2):
                    bank = (tb * 2 + half) % 4
                    fns = [lambda e, q=q, bank=bank, half=half, tb=tb: e.transpose(out=ps[bank][:, q * 128:(q + 1) * 128], in_=xt[:, half * 4 + q, tb * 128:(tb + 1) * 128], identity=k.ident_f)
                           for q in range(4)]
                    fw.group("tensor", fns, reads=[("xt", half * 4 + q) for q in range(4)] + [("consts", None)], writes=[("ps%d" % bank, None)])
                    dst = xin[:, tb % 2, half * 512:(half + 1) * 512]
                    eng = "vector" if ev % 2 == 0 else "scalar"
                    ev += 1
                    if eng == "vector":
                        fw.op("vector", lambda e, dst=dst, bank=bank: e.tensor_copy(out=dst, in_=ps[bank][:]), reads=[("ps%d" % bank, None)], writes=[("xin", tb % 2)])
                    else:
                        fw.op("scalar", lambda e, dst=dst, bank=bank: e.activation(out=dst, in_=ps[bank][:], func=AF.Copy), reads=[("ps%d" % bank, None)], writes=[("xin", tb % 2)])
                fw.dma("sync", k.d["out"][tok0 + tb * 128:tok0 + (tb + 1) * 128, :], xin[:, tb % 2, :], reads=[("xin", tb % 2)], writes=[("outd", None)])
    fw.barrier()
    al.release(mark)


def count_steps(k, gen_fn, it):
    k.fw.dry = True
    n = sum(1 for _ in gen_fn(it))
    k.fw.dry = False
    return n


def run_pipelined(k, n, front, back):
    if not k.overlap:
        for it in range(n):
            for _ in front(it):
                pass
            for _ in back(it):
                pass
        return
    for _ in front(0):
        pass
    for it in range(n):
        gb = back(it)
        if it + 1 < n:
            nb = count_steps(k, back, it)
            nf = count_steps(k, front, it + 1)
            gf = front(it + 1)
            ib = jf = 0
            while ib < nb or jf < nf:
                if jf >= nf or (ib < nb and ib * nf <= jf * nb):
                    next(gb, None)
                    ib += 1
                else:
                    next(gf, None)
                    jf += 1
            for _ in gb:
                pass
            for _ in gf:
                pass
        else:
            for _ in gb:
                pass


def phase_mixA(k, l):
    fw, nc, al, ps = k.fw, k.nc, k.al, k.ps
    mark = al.mark()
    cst = load_consts(k)
    NT = 256
    CH = 128
    NCH = NT // CH
    mixw = k.d["mixw"][l]
    Wu = al.sb("Wu", [128, 8, 1024], BF16)
    Wvm = al.sb("Wvm", [128, 8, 1024], BF16)
    Wom = al.sb("Wom", [128, 8, 1024], BF16)
    Wif32 = al.sb("Wif32", [128, 8, 8], F32)
    Wif = al.sb("Wif", [128, 8, 8], BF16)
    Wq = al.sb("Wq", [128, 2, 4, 256], BF16)
    Wk = al.sb("Wk", [128, 2, 4, 256], BF16)
    load_cast(k, Wu[:], mixw[:, :, 0:1024], "Wu")
    fw.dma("sync", Wif32[:], mixw[:, :, 3072:3080], writes=[("Wif32", None)])
    fw.op("vector", lambda e: e.tensor_copy(out=Wif[:], in_=Wif32[:]), reads=[("Wif32", None)], writes=[("Wif", None)])
    load_cast(k, Wom[:], mixw[:, :, 2048:3072], "Wom")
    load_cast(k, Wvm[:], mixw[:, :, 1024:2048], "Wvm")
    load_cast(k, Wq[:].rearrange("p a h e -> p a (h e)"), k.d["wq"][l], "Wq")
    load_cast(k, Wk[:].rearrange("p a h e -> p a (h e)"), k.d["wk"][l], "Wk")
    xts = [al.sb("xt%d" % i, [128, 8, NT], F32) for i in range(2)]
    hT = al.sb("hT", [128, 8, NT], BF16)
    lnv = al.sb("lnv", [128, NT], F32)
    rstd = al.sb("rstd", [128, NT], F32)
    hn = [al.sb("hn%d" % i, [128, NT], F32) for i in range(2)]
    u_sb = al.sb("u_sb", [128, 8, NT + 3], F32)
    acc = [al.sb("acc%d" % i, [128, NT], F32) for i in range(2)]
    gi = al.sb("gi", [4, NT], F32)
    ef = al.sb("ef", [4, NT], F32)
    Bext = al.sb("Bext", [4, NT + 1], F32)
    Et = al.sb("Et", [4, NT], F32)
    Mext = al.sb("Mext", [4, NT + 1], F32)
    g1 = al.sb("g1", [4, NT], F32)
    g2 = al.sb("g2", [4, NT], F32)
    dect = al.sb("dect", [4, 4], F32)
    ones4 = al.sb("ones4", [4, NT], F32)
    H = []
    for i in range(2):
        H.append(dict(
            uaT=al.sb("uaT%d" % i, [128, 8, NT], BF16), sigo=al.sb("sigo%d" % i, [128, 8, NT], BF16),
            qT=al.sb("qT%d" % i, [128, 8, NT], BF16), kT=al.sb("kT%d" % i, [128, 8, NT], BF16),
            ktm=al.sb("ktm%d" % i, [128, 2, 4, 256], BF16), vx=al.sb("vx%d" % i, [128, 2, 4, 384], BF16),
            clamp_rep=al.sb("clamp_rep%d" % i, [128, 4, NT], F32), dec_rep=al.sb("dec_rep%d" % i, [128, 4, 4], F32),
            kscT=al.sb("kscT%d" % i, [128, 2, 4], F32)))
    Cx = al.sb("Cx", [128, 4, 2, 384], BF16)
    SMs = [al.sb("SM%d" % i, [128, 4, CH], BF16) for i in range(NCH)]
    hraw = al.sb("hraw", [128, 4, 2, NT], F32)
    den = al.sb("den", [128, CH], F32)
    rec = al.sb("rec", [128, CH], F32)
    sqh = [al.sb("sqh%d" % i, [128, 2, NT], BF16) for i in range(2)]
    lnh = al.sb("lnh", [128, NT], F32)
    rsh = [al.sb("rsh%d" % i, [128, NT], F32) for i in range(2)]
    tA = [al.sb("tA%d" % i, [128, NT], F32) for i in range(2)]
    tB = [al.sb("tB%d" % i, [128, NT], F32) for i in range(2)]
    tC = [al.sb("tC%d" % i, [128, NT], F32) for i in range(2)]
    yaT = al.sb("yaT", [128, 8, NT], BF16)
    fw.op("vector", lambda e: e.memset(ones4[:], 1.0), writes=[("ones4", None)])
    fw.op("vector", lambda e: e.memset(dect[:], 0.0), writes=[("dect", None)])
    pcw = lambda m, j: k.pp[:, l, CW + m * 4 + j:CW + m * 4 + j + 1]

    def front(it):
        par = it % 2
        hb = H[par]
        uaT, sigo, qT, kT, ktm, vx, clamp_rep, dec_rep, kscT = (hb[n] for n in ("uaT", "sigo", "qT", "kT", "ktm", "vx", "clamp_rep", "dec_rep", "kscT"))
        P = "%d" % par
        b = (it * NT) // S
        tok0 = it * NT
        seq_start = (tok0 % S) == 0
        if seq_start:
            fw.op("gpsimd", lambda e: e.memset(u_sb[:, :, 0:3], 0.0), writes=[("u_sb", None)])
            fw.op("vector", lambda e: e.memset(Bext[:, 0:1], 0.0), writes=[("Bext", None)])
            fw.op("vector", lambda e: e.memset(Mext[:, 0:1], 0.0), writes=[("Mext", None)])
        else:
            fw.op("vector", lambda e: e.tensor_copy(out=Bext[:, 0:1], in_=Bext[:, NT:NT + 1]), reads=[("Bext", None)], writes=[("Bext", None)])
            fw.op("vector", lambda e: e.tensor_copy(out=Mext[:, 0:1], in_=Mext[:, NT:NT + 1]), reads=[("Mext", None)], writes=[("Mext", None)])
        xt = xts[par]
        if it == 0:
            fw.dma("sync", xt[:], k.d["xT"][:, :, tok0:tok0 + NT], writes=[("xt" + P, None)])
        if it + 1 < T // NT:
            fw.dma("sync", xts[1 - par][:], k.d["xT"][:, :, tok0 + NT:tok0 + 2 * NT], writes=[("xt%d" % (1 - par), None)])
        yield
        rms_h(k, xt, NT, b, l, 1, hT, (hT, "hT"), lnv, rstd, hn, ps[0], ("ps0", None), xkey="xt" + P)
        fw.dma("sync", k.d["hTs"][:, :, tok0:tok0 + NT], hT[:], reads=[("hT", None)], writes=[("hTsd", it)])
        for _ in range(3):
            yield
        mm(k, ps[0][0:4, 0:NT], ("ps0", None), [(Wif[:, kc, 0:4], hT[:, kc, :]) for kc in range(8)], reads=[("Wif", None), ("hT", None)])
        fw.op("scalar", lambda e: e.activation(out=gi[:], in_=ps[0][0:4, 0:NT], func=AF.Identity, bias=k.pp[0:4, l, GBI:GBI + 1], scale=1.0),
              reads=[("ps0", None), ("pp", None)], writes=[("gi", None)])
        mm(k, ps[0][0:4, 0:NT], ("ps0", None), [(Wif[:, kc, 4:8], hT[:, kc, :]) for kc in range(8)], reads=[("Wif", None), ("hT", None)])
        fw.op("scalar", lambda e: e.activation(out=ef[:], in_=ps[0][0:4, 0:NT], func=AF.Exp, bias=k.nbf[:, l:l + 1], scale=-1.0),
              reads=[("ps0", None), ("nbf", None)], writes=[("ef", None)])
        fw.op("scalar", lambda e: e.activation(out=ef[:], in_=ef[:], func=AF.Ln, bias=1.0, scale=1.0), reads=[("ef", None)], writes=[("ef", None)])
        fw.op("vector", lambda e: e.tensor_tensor_scan(out=Bext[:, 1:NT + 1], data0=ones4[:], data1=ef[:], initial=Bext[:, 0:1], op0=ALU.mult, op1=ALU.subtract),
              reads=[("ones4", None), ("ef", None), ("Bext", None)], writes=[("Bext", None)])
        fw.op("vector", lambda e: e.tensor_tensor(out=Et[:], in0=gi[:], in1=Bext[:, 1:NT + 1], op=ALU.subtract),
              reads=[("gi", None), ("Bext", None)], writes=[("Et", None)])
        fw.op("vector", lambda e: e.tensor_tensor_scan(out=Mext[:, 1:NT + 1], data0=ones4[:], data1=Et[:], initial=Mext[:, 0:1], op0=ALU.mult, op1=ALU.max),
              reads=[("ones4", None), ("Et", None), ("Mext", None)], writes=[("Mext", None)])
        R3 = Mext[:, 0:NT].rearrange("p (c q) -> p c q", q=CH)[:, :, 0:1]
        Rn3 = Mext[:, 1:NT + 1].rearrange("p (c q) -> p c q", q=CH)[:, :, CH - 1:CH]
        fw.op("vector", lambda e: e.tensor_tensor(out=g1[:].rearrange("p (c q) -> p c q", q=CH), in0=Et[:].rearrange("p (c q) -> p c q", q=CH), in1=bc(R3, [4, NCH, CH]), op=ALU.subtract),
              reads=[("Et", None), ("Mext", None)], writes=[("g1", None)])
        fw.op("scalar", lambda e: e.activation(out=g1[:], in_=g1[:], func=AF.Exp), reads=[("g1", None)], writes=[("g1", None)])
        fw.op("vector", lambda e: e.tensor_tensor(out=g2[:].rearrange("p (c q) -> p c q", q=CH), in0=Bext[:, 1:NT + 1].rearrange("p (c q) -> p c q", q=CH), in1=bc(R3, [4, NCH, CH]), op=ALU.add),
              reads=[("Bext", None), ("Mext", None)], writes=[("g2", None)])
        fw.op("scalar", lambda e: e.activation(out=g2[:], in_=g2[:], func=AF.Exp, scale=-1.0), reads=[("g2", None)], writes=[("g2", None)])
        fw.op("vector", lambda e: e.tensor_tensor(out=dect[:, 0:NCH].rearrange("p (c o) -> p c o", o=1), in0=R3, in1=Rn3, op=ALU.subtract),
              reads=[("Mext", None)], writes=[("dect", None)])
        fw.op("scalar", lambda e: e.activation(out=dect[:, 0:NCH], in_=dect[:, 0:NCH], func=AF.Exp), reads=[("dect", None)], writes=[("dect", None)])
        yield
        for mp in range(4):
            bank = mp % 2
            mm_multi(k, [(ps[bank][:, i * NT:(i + 1) * NT], [(Wu[:, kc, (2 * mp + i) * 128:(2 * mp + i + 1) * 128], hT[:, kc, :]) for kc in range(8)]) for i in range(2)],
                     reads=[("Wu", None), ("hT", None)], writes=[("ps%d" % bank, None)])
            fw.op("scalar", lambda e, mp=mp, bank=bank: e.activation(out=u_sb[:, 2 * mp:2 * mp + 2, 3:NT + 3], in_=ps[bank][:].rearrange("p (i t) -> p i t", t=NT), func=AF.Copy),
                  reads=[("ps%d" % bank, None)], writes=[("u_sb", 2 * mp), ("u_sb", 2 * mp + 1)])
            for m in (2 * mp, 2 * mp + 1):
                a_ = acc[m % 2]
                fw.op("vector", lambda e, m=m, a_=a_: e.tensor_scalar(out=a_[:], in0=u_sb[:, m, 0:NT], scalar1=pcw(m, 0), scalar2=None, op0=ALU.mult),
                      reads=[("u_sb", m), ("pp", None)], writes=[("acc", m % 2)])
                for j in range(1, 4):
                    fw.op("vector", lambda e, m=m, a_=a_, j=j: e.scalar_tensor_tensor(out=a_[:], in0=u_sb[:, m, j:NT + j], scalar=pcw(m, j), in1=a_[:], op0=ALU.mult, op1=ALU.add),
                          reads=[("u_sb", m), ("pp", None), ("acc", m % 2)], writes=[("acc", m % 2)])
                fw.op("scalar", lambda e, m=m, a_=a_: e.activation(out=uaT[:, m, :], in_=a_[:], func=AF.Silu, bias=k.pp[:, l, CB + m:CB + m + 1], scale=1.0),
                      reads=[("acc", m % 2), ("pp", None)], writes=[("uaT" + P, m)])
                yield
        fw.op("gpsimd", lambda e: e.tensor_copy(out=u_sb[:, :, 0:3], in_=u_sb[:, :, NT:NT + 3]), reads=[("u_sb", None)], writes=[("u_sb", None)])
        for mp in range(4):
            bank = mp % 2
            mm_multi(k, [(ps[bank][:, i * NT:(i + 1) * NT], [(Wom[:, kc, (2 * mp + i) * 128:(2 * mp + i + 1) * 128], hT[:, kc, :]) for kc in range(8)]) for i in range(2)],
                     reads=[("Wom", None), ("hT", None)], writes=[("ps%d" % bank, None)])
            fw.op("scalar", lambda e, mp=mp, bank=bank: e.activation(out=sigo[:, 2 * mp:2 * mp + 2, :], in_=ps[bank][:].rearrange("p (i t) -> p i t", t=NT), func=AF.Sigmoid),
                  reads=[("ps%d" % bank, None)], writes=[("sigo" + P, 2 * mp), ("sigo" + P, 2 * mp + 1)])
            yield
            yield
        fw.group("tensor", [lambda e, tt=tt: e.transpose(out=ps[0][:, tt * 4:tt * 4 + 4], in_=g1[0:4, tt * 128:(tt + 1) * 128], identity=cst[0:4, IDENT:IDENT + 4]) for tt in range(2)],
                 reads=[("g1", None), ("consts", None)], writes=[("ps0", None)])
        fw.op("vector", lambda e: e.tensor_copy(out=kscT[:], in_=ps[0][:, 0:8].rearrange("p (t h) -> p t h", h=4)), reads=[("ps0", None)], writes=[("kscT" + P, None)])
        mm_multi(k, [(ps[0][:, 16 + h * NCH:16 + (h + 1) * NCH], [(cst[0:4, SEL + h * 128:SEL + (h + 1) * 128], dect[0:4, 0:NCH])]) for h in range(4)],
                 reads=[("dect", None), ("consts", None)], writes=[("ps0", None)])
        fw.op("vector", lambda e: e.tensor_copy(out=dec_rep[:, :, 0:NCH], in_=ps[0][:, 16:16 + 4 * NCH].rearrange("p (h c) -> p h c", c=NCH)), reads=[("ps0", None)], writes=[("dec_rep" + P, None)])
        for hp2 in range(2):
            bank = hp2
            mm_multi(k, [(ps[bank][:, hh * NT:(hh + 1) * NT], [(cst[0:4, SEL + (2 * hp2 + hh) * 128:SEL + (2 * hp2 + hh + 1) * 128], g2[0:4, :])]) for hh in range(2)],
                     reads=[("g2", None), ("consts", None)], writes=[("ps%d" % bank, None)])
            fw.op("scalar", lambda e, hp2=hp2, bank=bank: e.activation(out=clamp_rep[:, 2 * hp2:2 * hp2 + 2, :], in_=ps[bank][:].rearrange("p (a t) -> p a t", t=NT), func=AF.Copy),
                  reads=[("ps%d" % bank, None)], writes=[("clamp_rep" + P, None)])
        yield
        for tt in range(2):
            for half in range(2):
                bank = half
                mm(k, ps[bank][:], ("ps%d" % bank, None), [(hT[:, kc, tt * 128:(tt + 1) * 128], Wvm[:, kc, half * 512:(half + 1) * 512]) for kc in range(8)], reads=[("Wvm", None), ("hT", None)])
                for hh in range(2):
                    h = half * 2 + hh
                    fw.op("vector", lambda e, tt=tt, h=h, hh=hh, bank=bank: e.tensor_scalar(out=vx[:, tt, h, 0:256], in0=ps[bank][:, hh * 256:(hh + 1) * 256], scalar1=kscT[:, tt, h:h + 1], scalar2=None, op0=ALU.mult),
                          reads=[("ps%d" % bank, None), ("kscT" + P, None)], writes=[("vx" + P, tt * 4 + h)])
                yield
            fw.op("gpsimd", lambda e, tt=tt: e.tensor_tensor(out=vx[:, tt, :, 256:384], in0=bc(cst[:, ONES:ONES + 128].rearrange("p (o c) -> p o c", o=1), [128, 4, 128]),
                                                         in1=bc(kscT[:, tt, :].rearrange("p (h o) -> p h o", o=1), [128, 4, 128]), op=ALU.mult),
                  reads=[("consts", None), ("kscT" + P, None)], writes=[("vx" + P, tt * 4 + h) for h in range(4)])
        for h in range(4):
            for ec in range(2):
                bank = ec
                mm(k, ps[bank][:, 0:NT], ("ps%d" % bank, None), [(Wq[:, dc, h, ec * 128:(ec + 1) * 128], uaT[:, 2 * h + dc, :]) for dc in range(2)], reads=[("Wq", None), ("uaT" + P, 2 * h), ("uaT" + P, 2 * h + 1)])
                fw.op("scalar", lambda e, h=h, ec=ec, bank=bank: e.activation(out=qT[:, 2 * h + ec, :], in_=ps[bank][:, 0:NT], func=AF.Copy), reads=[("ps%d" % bank, None)], writes=[("qT" + P, 2 * h + ec)])
                mm(k, ps[bank][:, NT:2 * NT], ("ps%d" % bank, None), [(Wk[:, dc, h, ec * 128:(ec + 1) * 128], uaT[:, 2 * h + dc, :]) for dc in range(2)], reads=[("Wk", None), ("uaT" + P, 2 * h), ("uaT" + P, 2 * h + 1)])
                fw.op("vector", lambda e, h=h, ec=ec, bank=bank: e.tensor_scalar(out=kT[:, 2 * h + ec, :], in0=ps[bank][:, NT:2 * NT], scalar1=0.0625, scalar2=None, op0=ALU.mult), reads=[("ps%d" % bank, None)], writes=[("kT" + P, 2 * h + ec)])
            yield
        for tt in range(2):
            for h in range(4):
                bank = h % 2
                mm(k, ps[bank][:, 0:256], ("ps%d" % bank, None), [(uaT[:, 2 * h + dc, tt * 128:(tt + 1) * 128], Wk[:, dc, h, :]) for dc in range(2)], reads=[("Wk", None), ("uaT" + P, 2 * h), ("uaT" + P, 2 * h + 1)])
                if h % 2 == 0:
                    fw.op("scalar", lambda e, tt=tt, h=h, bank=bank: e.activation(out=ktm[:, tt, h, :], in_=ps[bank][:, 0:256], func=AF.Copy, scale=0.0625), reads=[("ps%d" % bank, None)], writes=[("ktm" + P, tt * 4 + h)])
                else:
                    fw.op("vector", lambda e, tt=tt, h=h, bank=bank: e.tensor_scalar(out=ktm[:, tt, h, :], in0=ps[bank][:, 0:256], scalar1=0.0625, scalar2=None, op0=ALU.mult), reads=[("ps%d" % bank, None)], writes=[("ktm" + P, tt * 4 + h)])
            yield

    def back(it):
        par = it % 2
        hb = H[par]
        uaT, sigo, qT, kT, ktm, vx, clamp_rep, dec_rep, kscT = (hb[n] for n in ("uaT", "sigo", "qT", "kT", "ktm", "vx", "clamp_rep", "dec_rep", "kscT"))
        P = "%d" % par
        tok0 = it * NT
        seq_start = (tok0 % S) == 0
        seq_last_tile = ((tok0 + NT) % S) == 0
        if seq_start:
            fw.op("gpsimd", lambda e: e.memset(Cx[:], 0.0), writes=[("Cx", None)])
        for cl in range(NCH):
            cols = slice(cl * CH, (cl + 1) * CH)
            SM = SMs[cl]
            mm_multi(k, [(ps[3][:, h * CH:(h + 1) * CH], [(kT[:, 2 * h + ec, cols], qT[:, 2 * h + ec, cols]) for ec in range(2)]) for h in range(4)],
                     reads=[("kT" + P, None), ("qT" + P, None)], writes=[("ps3", None)])
            fw.op("vector", lambda e, SM=SM: e.tensor_tensor(out=SM[:], in0=ps[3][:].rearrange("p (h t) -> p h t", t=CH),
                                                        in1=bc(cst[:, MASK2:MASK2 + CH].rearrange("p (o t) -> p o t", o=1), [128, 4, CH]), op=ALU.mult),
                  reads=[("ps3", None), ("consts", None)], writes=[("SM%d" % cl, None)])
            yield
        for cl in range(NCH):
            tt = cl
            cols = slice(cl * CH, (cl + 1) * CH)
            last_chunk = seq_last_tile and cl == NCH - 1
            SM = SMs[cl]
            def emit_out(h):
                ob = ps[4 + h % 2]
                okey = "ps%d" % (4 + h % 2)
                groups = []
                for j in range(3):
                    prs = [(Cx[:, h, dkc, j * 128:(j + 1) * 128], qT[:, 2 * h + dkc, cols]) for dkc in range(2)]
                    prs.append((vx[:, tt, h, j * 128:(j + 1) * 128], SM[:, h, :]))
                    groups.append((ob[:, j * CH:(j + 1) * CH], prs))
                mm_multi(k, groups, reads=[("Cx", h), ("qT" + P, 2 * h), ("qT" + P, 2 * h + 1), ("vx" + P, tt * 4 + h), ("SM%d" % cl, None)], writes=[(okey, None)])
                ob3 = ob[:, 0:3 * CH].rearrange("p (j t) -> p j t", t=CH)
                fw.op("scalar", lambda e, ob3=ob3: e.activation(out=den[:], in_=ob3[:, 2, :], func=AF.Abs), reads=[(okey, None)], writes=[("den", None)])
                fw.op("vector", lambda e, h=h, cols=cols: e.tensor_tensor(out=den[:], in0=den[:], in1=clamp_rep[:, h, cols], op=ALU.max),
                      reads=[("den", None), ("clamp_rep" + P, None)], writes=[("den", None)])
                fw.op("vector", lambda e: e.reciprocal(out=rec[:], in_=den[:]), reads=[("den", None)], writes=[("rec", None)])
                fw.op("vector", lambda e, ob3=ob3, h=h, cols=cols: e.tensor_tensor(out=hraw[:, h, :, cols], in0=ob3[:, 0:2, :], in1=bc(rec[:].rearrange("p (o t) -> p o t", o=1), [128, 2, CH]), op=ALU.mult),
                      reads=[(okey, None), ("rec", None)], writes=[("hraw", None)])

            def emit_U(h):
                ub = (6, 7) if h % 2 == 0 else (2, 3)
                for dkc in range(2):
                    U = ps[ub[dkc]]
                    mm(k, U[:, 0:384], ("ps%d" % ub[dkc], None),
                       [(k.ident_bf[:], Cx[:, h, dkc, :]), (ktm[:, tt, h, dkc * 128:(dkc + 1) * 128], vx[:, tt, h, :])],
                       reads=[("ident_bf", None), ("Cx", h), ("ktm" + P, tt * 4 + h), ("vx" + P, tt * 4 + h)])
                for dkc in range(2):
                    U = ps[ub[dkc]]
                    dsc = dec_rep[:, h, cl:cl + 1]
                    if dkc == 0:
                        fw.op("scalar", lambda e, U=U, h=h, dkc=dkc, dsc=dsc: e.activation(out=Cx[:, h, dkc, :], in_=U[:, 0:384], func=AF.Identity, scale=dsc, bias=0.0),
                              reads=[("ps%d" % ub[dkc], None), ("dec_rep" + P, None), ("Cx", h)], writes=[("Cx", h)])
                    else:
                        fw.op("vector", lambda e, U=U, h=h, dkc=dkc, dsc=dsc: e.tensor_scalar(out=Cx[:, h, dkc, :], in0=U[:, 0:384], scalar1=dsc, scalar2=None, op0=ALU.mult),
                              reads=[("ps%d" % ub[dkc], None), ("dec_rep" + P, None), ("Cx", h)], writes=[("Cx", h)])

            order = [("o", 0), ("o", 1), ("u", 0), ("o", 2), ("u", 1), ("o", 3), ("u", 2), ("u", 3)]
            for kind, h in order:
                if kind == "o":
                    emit_out(h)
                    yield
                elif not last_chunk:
                    emit_U(h)
                    yield
        for h in range(4):
            sq_ = sqh[h % 2]
            fw.op("scalar", lambda e, h=h, sq_=sq_: e.activation(out=sq_[:], in_=hraw[:, h], func=AF.Square), reads=[("hraw", None)], writes=[("sqh", h % 2)])
            mm(k, ps[3][:, 256:512], ("ps3", None), [(k.ones_bf[:], sq_[:, j, :]) for j in range(2)], reads=[("sqh", h % 2), ("ones_bf", None)])
            fw.op("scalar", lambda e: e.activation(out=lnh[:], in_=ps[3][:, 256:512], func=AF.Ln, scale=1.0 / 256, bias=EPS), reads=[("ps3", None)], writes=[("lnh", None)])
            rs_ = rsh[h % 2]
            fw.op("scalar", lambda e, rs_=rs_: e.activation(out=rs_[:], in_=lnh[:], func=AF.Exp, scale=-0.5), reads=[("lnh", None)], writes=[("rsh", h % 2)])
            for j in range(2):
                m = 2 * h + j
                a_, b_, c_ = tA[m % 2], tB[m % 2], tC[m % 2]
                fw.op("vector", lambda e, h=h, j=j, a_=a_, rs_=rs_: e.tensor_tensor(out=a_[:], in0=hraw[:, h, j, :], in1=rs_[:], op=ALU.mult),
                      reads=[("hraw", None), ("rsh", h % 2)], writes=[("tA", m % 2)])
                fw.op("vector", lambda e, m=m, a_=a_, b_=b_: e.scalar_tensor_tensor(out=b_[:], in0=a_[:], scalar=k.pp[:, l, ONORM + m:ONORM + m + 1], in1=sigo[:, m, :], op0=ALU.mult, op1=ALU.mult),
                      reads=[("tA", m % 2), ("pp", None), ("sigo" + P, m)], writes=[("tB", m % 2)])
                fw.op("vector", lambda e, m=m, c_=c_: e.scalar_tensor_tensor(out=c_[:], in0=uaT[:, m, :], scalar=k.pp[:, l, SKIP + m:SKIP + m + 1], in1=sigo[:, m, :], op0=ALU.mult, op1=ALU.mult),
                      reads=[("uaT" + P, m), ("pp", None), ("sigo" + P, m)], writes=[("tC", m % 2)])
                fw.op("gpsimd", lambda e, m=m, b_=b_, c_=c_: e.tensor_tensor(out=yaT[:, m, :], in0=b_[:], in1=c_[:], op=ALU.add),
                      reads=[("tB", m % 2), ("tC", m % 2)], writes=[("yaT", m)])
            yield
        fw.dma("sync", k.d["yaT"][:, :, tok0:tok0 + NT], yaT[:], reads=[("yaT", None)], writes=[("yaTd", None)])

    run_pipelined(k, T // NT, front, back)
    fw.barrier()
    al.release(mark)


def phase_mixB(k, l):
    fw, nc, al, ps = k.fw, k.nc, k.al, k.ps
    mark = al.mark()
    cst = load_consts(k)
    NT = 512
    NS = NT // 128
    mixw = k.d["mixw"][l]
    Wqa = al.sb("Wqa", [128, 8, 1024], BF16)
    Wka = al.sb("Wka", [128, 8, 512], BF16)
    Wva = al.sb("Wva", [128, 8, 256], BF16)
    load_cast(k, Wqa[:], mixw[:, :, 3080:4104], "Wqa")
    load_cast(k, Wka[:], k.d["wka_dup"][l], "Wka")
    load_cast(k, Wva[:], mixw[:, :, 4360:4616], "Wva")
    xt = al.sb("xt", [128, 8, NT], F32)
    hT = al.sb("hT", [128, 8, NT], BF16)
    lnv = al.sb("lnv", [128, NT], F32)
    rstd = al.sb("rstd", [128, NT], F32)
    hn = [al.sb("hn%d" % i, [128, NT], F32) for i in range(2)]
    cosT = al.sb("cosT", [128, NT], F32)
    sinT = al.sb("sinT", [128, NT], F32)
    ND = 4
    sq2 = [al.sb("sq2_%d" % i, [128, NT], BF16) for i in range(ND)]
    ln2 = [al.sb("ln2_%d" % i, [128, NT], F32) for i in range(2)]
    rs2 = [al.sb("rs2_%d" % i, [128, NT], F32) for i in range(ND)]
    qnw = [al.sb("qnw%d" % i, [128, NT], F32) for i in range(ND)]
    r1 = [al.sb("r1_%d" % i, [128, NT], F32) for i in range(ND)]
    r2 = [al.sb("r2_%d" % i, [128, NT], F32) for i in range(ND)]
    H = []
    for i in range(2):
        H.append(dict(qlo=al.sb("qlo%d" % i, [128, 8, NT], BF16), qhi=al.sb("qhi%d" % i, [128, 8, NT], BF16),
                      krT=al.sb("krT%d" % i, [128, 4, 128 + NT], BF16), vh=al.sb("vh%d" % i, [128, NS + 1, 2, 4, 128], BF16)))
        fw.op("gpsimd", lambda e, i=i: e.memset(H[i]["vh"][:], 0.0), writes=[("vdup%d" % i, None)])
        fw.op("gpsimd", lambda e, i=i: e.memset(H[i]["qlo"][64:128], 0.0), writes=[("qlo%d" % i, None)])
        fw.op("gpsimd", lambda e, i=i: e.memset(H[i]["qhi"][0:64], 0.0), writes=[("qhi%d" % i, None)])
    pT = [al.sb("pT%d" % i, [128, 2, 256], BF16) for i in range(3)]
    ones_tb = al.sb("ones_tb", [128, 2, 128], BF16)
    fw.op("vector", lambda e: e.memset(ones_tb[:], 0.0), writes=[("ones_tb", None)])
    fw.op("vector", lambda e: e.memset(ones_tb[:, 0, 0:64], 1.0), writes=[("ones_tb", None)])
    fw.op("vector", lambda e: e.memset(ones_tb[:, 1, 64:128], 1.0), writes=[("ones_tb", None)])
    es2 = al.sb("es2", [2, 8], F32)
    sinkrow = al.sb("sinkrow", [2, 4, 128], F32)
    fw.op("scalar", lambda e: e.activation(out=es2[:], in_=k.pp[0:2, l, SINK2:SINK2 + 8], func=AF.Exp), reads=[("pp", None)], writes=[("es2", None)])
    fw.op("vector", lambda e: e.tensor_copy(out=sinkrow[:].rearrange("p v (g q) -> p (v g) q", q=64), in_=bc(es2[:].rearrange("p (m o) -> p m o", o=1), [2, 8, 64])),
          reads=[("es2", None)], writes=[("sinkrow", None)])
    ebias = al.sb("ebias", [128, 3], F32)
    fw.op("vector", lambda e: e.memset(ebias[:], 0.0), writes=[("ebias", None)])
    fw.op("vector", lambda e: e.memset(ebias[64:128, 1:2], -10000.0), writes=[("ebias", None)])
    fw.op("vector", lambda e: e.memset(ebias[0:64, 2:3], -10000.0), writes=[("ebias", None)])
    dsum = al.sb("dsum", [128, 2, 64], F32)
    rec = al.sb("recb", [128, 2, 64], F32)
    ybT = al.sb("ybT", [128, 8, NT], BF16)

    def front(it):
        par = it % 2
        P = "%d" % par
        qlo, qhi, krT, vh = (H[par][n] for n in ("qlo", "qhi", "krT", "vh"))
        krT_prev, vh_prev = H[1 - par]["krT"], H[1 - par]["vh"]
        b = (it * NT) // S
        tok0 = it * NT
        seq_start = (tok0 % S) == 0
        fw.dma("sync", hT[:], k.d["hTs"][:, :, tok0:tok0 + NT], writes=[("hT", None)])
        fw.dma("sync", cosT[:], k.d["cosT"][:, tok0:tok0 + NT], writes=[("cosT", None)])
        fw.dma("sync", sinT[:], k.d["sinT"][:, tok0:tok0 + NT], writes=[("sinT", None)])
        if not seq_start:
            fw.op("gpsimd", lambda e: e.tensor_copy(out=krT[:, :, 0:128], in_=krT_prev[:, :, NT:NT + 128]), reads=[("krT%d" % (1 - par), None)], writes=[("krT" + P, None)])
            fw.op("gpsimd", lambda e: e.tensor_copy(out=vh[:, 0], in_=vh_prev[:, NS]), reads=[("vdup%d" % (1 - par), None)], writes=[("vdup" + P, None)])
        yield

        def qk_part1(W_lhs_fn, wkey, wcol, is_q, idx, blk):
            i4 = blk % ND
            bank = 1 + blk % 2
            mm(k, ps[bank][:, 0:NT], ("ps%d" % bank, None), [(W_lhs_fn(kc), hT[:, kc, :]) for kc in range(8)], reads=[(wkey, None), ("hT", None)])
            fw.op("scalar", lambda e: e.activation(out=sq2[i4][:], in_=ps[bank][:, 0:NT], func=AF.Square), reads=[("ps%d" % bank, None)], writes=[("sq2", i4)])
            fw.op("scalar", lambda e: e.activation(out=qnw[i4][:], in_=ps[bank][:, 0:NT], func=AF.Identity, scale=k.pp[:, l, wcol:wcol + 1], bias=0.0),
                  reads=[("ps%d" % bank, None), ("pp", None)], writes=[("qnw", i4)])

        def qk_part2(W_lhs_fn, wkey, wcol, is_q, idx, blk):
            i4 = blk % ND
            i2 = blk % 2
            bank = 1 + i2
            qb = 0
            mm(k, ps[qb][:, 0:NT], ("ps%d" % qb, None), [(k.blk_bf[:], sq2[i4][:])], reads=[("sq2", i4), ("blk_bf", None)])
            mm(k, ps[bank][:, 0:NT], ("ps%d" % bank, None), [(cst[:, ROT:ROT + 128], qnw[i4][:])], reads=[("qnw", i4), ("consts", None)])
            fw.op("scalar", lambda e: e.activation(out=ln2[i2][:], in_=ps[qb][:, 0:NT], func=AF.Ln, scale=1.0 / 64, bias=EPS), reads=[("ps%d" % qb, None)], writes=[("ln2", i2)])
            fw.op("scalar", lambda e: e.activation(out=rs2[i4][:], in_=ln2[i2][:], func=AF.Exp, scale=-0.5), reads=[("ln2", i2)], writes=[("rs2", i4)])
            fw.op("gpsimd", lambda e: e.tensor_tensor(out=r1[i4][:], in0=qnw[i4][:], in1=cosT[:], op=ALU.mult), reads=[("qnw", i4), ("cosT", None)], writes=[("r1", i4)])
            fw.op("vector", lambda e: e.tensor_tensor(out=r2[i4][:], in0=ps[bank][:, 0:NT], in1=sinT[:], op=ALU.mult), reads=[("ps%d" % bank, None), ("sinT", None)], writes=[("r2", i4)])
            fw.op("gpsimd", lambda e: e.tensor_tensor(out=r1[i4][:], in0=r1[i4][:], in1=r2[i4][:], op=ALU.add), reads=[("r1", i4), ("r2", i4)], writes=[("r1", i4)])
            if is_q:
                fw.op("vector", lambda e: e.tensor_tensor(out=qlo[0:64, idx, :], in0=r1[i4][0:64], in1=rs2[i4][0:64], op=ALU.mult), reads=[("r1", i4), ("rs2", i4)], writes=[("qlo" + P, idx)])
                fw.op("gpsimd", lambda e: e.tensor_tensor(out=qhi[64:128, idx, :], in0=r1[i4][64:128], in1=rs2[i4][64:128], op=ALU.mult), reads=[("r1", i4), ("rs2", i4)], writes=[("qhi" + P, idx)])
            else:
                fw.op("vector", lambda e: e.tensor_tensor(out=krT[:, idx, 128:128 + NT], in0=r1[i4][:], in1=rs2[i4][:], op=ALU.mult), reads=[("r1", i4), ("rs2", i4)], writes=[("krT" + P, None)])

        blocks = [(lambda kc, kv=kv: Wka[:, kc, kv * 128:(kv + 1) * 128], "Wka", KW, False, kv) for kv in range(4)]
        blocks += [(lambda kc, m=m: Wqa[:, kc, m * 128:(m + 1) * 128], "Wqa", QW, True, m) for m in range(8)]
        for bi in range(len(blocks)):
            qk_part1(*blocks[bi], bi)
            qk_part2(*blocks[bi], bi)
            yield
            if bi == 3:
                for tt in range(NS):
                    bank = 1 + tt % 2
                    mm(k, ps[bank][:, 0:256], ("ps%d" % bank, None), [(hT[:, kc, tt * 128:(tt + 1) * 128], Wva[:, kc, :]) for kc in range(8)], reads=[("Wva", None), ("hT", None)])
                    for dup in range(2):
                        fw.op("scalar", lambda e, tt=tt, bank=bank, dup=dup: e.activation(out=vh[:, 1 + tt, dup, :, dup * 64:(dup + 1) * 64], in_=ps[bank][:, 0:256].rearrange("p (v d) -> p v d", d=64), func=AF.Copy),
                              reads=[("ps%d" % bank, None)], writes=[("vdup" + P, None)])
                yield

    def back(it):
        par = it % 2
        P = "%d" % par
        qlo, qhi, krT, vdup = (H[par][n] for n in ("qlo", "qhi", "krT", "vh"))
        tok0 = it * NT
        c0 = (tok0 % S) // 64
        steps = []
        for cl in range(NT // 64):
            c = c0 + cl
            subs = {}
            for kc_ in (c - 2, c - 1, c):
                if kc_ < 0:
                    continue
                rel = kc_ - c0
                subs.setdefault((128 + rel * 64) // 128, []).append(rel % 2)
            info = []
            for slot, vs in enumerate(sorted(subs)):
                hv = subs[vs]
                bias = 0 if len(hv) == 2 else (1 if hv[0] == 0 else 2)
                info.append((slot, vs, bias))
            for kv in range(4):
                steps.append((cl, kv, info))

        def emit_S(i):
            cl, kv, info = steps[i]
            qcols = slice(cl * 64, (cl + 1) * 64)
            sb_ = ps[3 + i % 3]
            groups = []
            for (slot, vs, bias) in info:
                kcol = slice(vs * 128, vs * 128 + 128)
                groups.append((sb_[:, slot * 256:slot * 256 + 128], [(krT[:, kv, kcol], qlo[:, 2 * kv:2 * kv + 2, qcols])]))
                groups.append((sb_[:, slot * 256 + 128:slot * 256 + 256], [(krT[:, kv, kcol], qhi[:, 2 * kv:2 * kv + 2, qcols])]))
            mm_multi(k, groups, reads=[("krT" + P, None), ("qlo" + P, 2 * kv), ("qlo" + P, 2 * kv + 1), ("qhi" + P, 2 * kv), ("qhi" + P, 2 * kv + 1)], writes=[("ps%d" % (3 + i % 3), None)])

        def emit_rest(i):
            cl, kv, info = steps[i]
            qcols = slice(cl * 64, (cl + 1) * 64)
            sb_ = ps[3 + i % 3]
            skey = "ps%d" % (3 + i % 3)
            nb_ = ps[6 + i % 2]
            nkey = "ps%d" % (6 + i % 2)
            p_ = pT[i % 3]
            pkey = ("pT", i % 3)
            for (slot, vs, bias) in info:
                fw.op("scalar", lambda e, slot=slot, sb_=sb_, p_=p_, bias=bias: e.activation(out=p_[:, slot, :], in_=sb_[:, slot * 256:(slot + 1) * 256], func=AF.Exp, scale=0.125, bias=ebias[:, bias:bias + 1]),
                      reads=[(skey, None), ("ebias", None)], writes=[pkey])
            prs_n, prs_d = [], []
            for (slot, vs, bias) in info:
                prs_n.append((vdup[:, vs, 0, kv, :], p_[:, slot, 0:128]))
                prs_n.append((vdup[:, vs, 1, kv, :], p_[:, slot, 128:256]))
                prs_d.append((ones_tb[:, 0, :], p_[:, slot, 0:128]))
                prs_d.append((ones_tb[:, 1, :], p_[:, slot, 128:256]))
            prs_d.append((cst[0:2, HSEL:HSEL + 128], sinkrow[0:2, kv, :]))
            mm_multi(k, [(nb_[:, 0:128], prs_n), (nb_[:, 128:256], prs_d)], reads=[("vdup" + P, None), pkey, ("ones_tb", None), ("sinkrow", None), ("consts", None)], writes=[(nkey, None)])
            fw.op("vector", lambda e, nb_=nb_: e.reciprocal(out=rec[:], in_=nb_[:, 128:256].rearrange("p (g q) -> p g q", q=64)), reads=[(nkey, None)], writes=[("recb", None)])
            fw.op("vector", lambda e, nb_=nb_, kv=kv, qcols=qcols: e.tensor_tensor(out=ybT[:, 2 * kv:2 * kv + 2, qcols], in0=nb_[:, 0:128].rearrange("p (g q) -> p g q", q=64), in1=rec[:], op=ALU.mult),
                  reads=[(nkey, None), ("recb", None)], writes=[("ybT", 2 * kv), ("ybT", 2 * kv + 1)])

        emit_S(0)
        emit_S(1)
        for i in range(len(steps)):
            if i + 2 < len(steps):
                emit_S(i + 2)
            emit_rest(i)
            yield
        fw.dma("sync", k.d["ybT"][:, :, tok0:tok0 + NT], ybT[:], reads=[("ybT", None)], writes=[("ybTd", None)])

    run_pipelined(k, T // NT, front, back)
    fw.barrier()
    al.release(mark)


def phase_mixC(k, l):
    fw, nc, al, ps = k.fw, k.nc, k.al, k.ps
    mark = al.mark()
    NT = 512
    Pa = al.sb("Pa", [128, 8, 1024], BF16)
    Pb = al.sb("Pb", [128, 8, 1024], BF16)
    Wm = al.sb("Wm", [128, 8, 2048], BF16)
    Wo = al.sb("Wo", [128, 8, 1024], BF16)
    load_cast(k, Wm[:], k.d["merge_w"][l], "Wm")
    load_cast(k, Pa[:], k.d["proj_a"][l], "Pa")
    load_cast(k, Pb[:], k.d["proj_b"][l], "Pb")
    load_cast(k, Wo[:], k.d["w_out"][l], "Wo")
    lnv = al.sb("lnv", [128, NT], F32)
    rstd = al.sb("rstd", [128, NT], F32)
    hn = [al.sb("hn%d" % i, [128, NT], F32) for i in range(2)]
    H = []
    for i in range(2):
        H.append(dict(xt=al.sb("xt%d" % i, [128, 8, NT], F32), ya=al.sb("ya%d" % i, [128, 8, NT], BF16), yb=al.sb("yb%d" % i, [128, 8, NT], BF16),
                      hT=al.sb("hT%d" % i, [128, 8, NT], BF16)))
    gab = al.sb("gab", [128, 16, NT], BF16)
    m1 = al.sb("m1", [128, NT], F32)
    m2 = al.sb("m2", [128, NT], F32)
    mg = al.sb("mg", [128, 8, NT], BF16)

    def front(it):
        par = it % 2
        P = "%d" % par
        xt, ya, yb, hT = (H[par][n] for n in ("xt", "ya", "yb", "hT"))
        b = (it * NT) // S
        tok0 = it * NT
        fw.dma("sync", xt[:], k.d["xT"][:, :, tok0:tok0 + NT], writes=[("xt" + P, None)])
        fw.dma("sync", ya[:], k.d["yaT"][:, :, tok0:tok0 + NT], writes=[("ya" + P, None)])
        fw.dma("sync", yb[:], k.d["ybT"][:, :, tok0:tok0 + NT], writes=[("yb" + P, None)])
        fw.dma("sync", hT[:], k.d["hTs"][:, :, tok0:tok0 + NT], writes=[("hT" + P, None)])
        yield

    def back(it):
        par = it % 2
        P = "%d" % par
        xt, ya, yb, hT = (H[par][n] for n in ("xt", "ya", "yb", "hT"))
        b = (it * NT) // S
        tok0 = it * NT
        for o in range(16):
            bank = 1 + o % 2
            mm(k, ps[bank][:, 0:NT], ("ps%d" % bank, None), [(Wm[:, kc, o * 128:(o + 1) * 128], hT[:, kc, :]) for kc in range(8)], reads=[("Wm", None), ("hT" + P, None)])
            fw.op("scalar", lambda e, o=o, bank=bank: e.activation(out=gab[:, o, :], in_=ps[bank][:, 0:NT], func=AF.Sigmoid, bias=k.pp[:, l, MB + o:MB + o + 1], scale=1.0),
                  reads=[("ps%d" % bank, None), ("pp", None)], writes=[("gab", o)])
            if o % 4 == 3:
                yield
        for o in range(8):
            ba, bb = (3, 4) if o % 2 == 0 else (7, 0)
            mm(k, ps[ba][:, 0:NT], ("ps%d" % ba, None), [(Pa[:, m, o * 128:(o + 1) * 128], ya[:, m, :]) for m in range(8)], reads=[("Pa", None), ("ya" + P, None)])
            mm(k, ps[bb][:, 0:NT], ("ps%d" % bb, None), [(Pb[:, m, o * 128:(o + 1) * 128], yb[:, m, :]) for m in range(8)], reads=[("Pb", None), ("yb" + P, None)])
            fw.op("vector", lambda e, o=o, ba=ba: e.tensor_tensor(out=m1[:], in0=ps[ba][:, 0:NT], in1=gab[:, o, :], op=ALU.mult), reads=[("ps%d" % ba, None), ("gab", o)], writes=[("m1", None)])
            fw.op("vector", lambda e, o=o, bb=bb: e.tensor_tensor(out=m2[:], in0=ps[bb][:, 0:NT], in1=gab[:, 8 + o, :], op=ALU.mult), reads=[("ps%d" % bb, None), ("gab", 8 + o)], writes=[("m2", None)])
            fw.op("gpsimd", lambda e, o=o: e.tensor_tensor(out=mg[:, o, :], in0=m1[:], in1=m2[:], op=ALU.add), reads=[("m1", None), ("m2", None)], writes=[("mg", o)])
            if o % 2 == 1:
                yield
        for o in range(8):
            bank = 5 + o % 2
            mm(k, ps[bank][:, 0:NT], ("ps%d" % bank, None), [(Wo[:, m, o * 128:(o + 1) * 128], mg[:, m, :]) for m in range(8)], reads=[("Wo", None)] + [("mg", m) for m in range(8)])
            G = k.der[:, l, 5, o, b:b + 1]
            fw.op("vector", lambda e, o=o, bank=bank, G=G: e.scalar_tensor_tensor(out=xt[:, o, :], in0=ps[bank][:, 0:NT], scalar=G, in1=xt[:, o, :], op0=ALU.mult, op1=ALU.add),
                  reads=[("ps%d" % bank, None), ("xt" + P, o), ("der", None)], writes=[("xt" + P, o)])
            if o % 2 == 1:
                yield
        fw.dma("sync", k.d["xT"][:, :, tok0:tok0 + NT], xt[:], reads=[("xt" + P, None)], writes=[("xTd", None)])

    run_pipelined(k, T // NT, front, back)
    fw.barrier()
    al.release(mark)


def build_program(n_layers=L, stop=None, dbg=None, overlap=True):
    nc = bass.Bass("TRN2", target_bir_lowering=False)
    k = K()
    k.nc = nc
    k.fw = FW(nc)
    k.al = Alloc(nc)
    k.overlap = overlap
    d = {}

    def din(name, shape, dt=F32):
        d[name] = nc.dram_tensor(name, shape, dt, kind="ExternalInput").ap()
    din("x", [128, 8, T])
    din("cT", [128, 8, 2])
    din("pos", [NBC, S], I32)
    din("ada_w", [L, 128, 8, 9 * D])
    din("pp", [128, L, NPP])
    din("consts", [128, NCONST])
    din("f1_win", [L, 128, 8 * 2 * FF])
    din("f1_wout", [L, 128, FC * D])
    din("f2_win", [L, 128, 8 * 2 * FF])
    din("f2_wout", [L, 128, FC * D])
    din("mixw", [L, 128, 8, NIN])
    din("wka_dup", [L, 128, 8, 512])
    din("wq", [L, 128, 2, 1024])
    din("wk", [L, 128, 2, 1024])
    din("proj_a", [L, 128, 8, D])
    din("proj_b", [L, 128, 8, D])
    din("merge_w", [L, 128, 8, 2 * D])
    din("w_out", [L, 128, 8, D])
    d["out"] = nc.dram_tensor("out", [128, 8, T], F32, kind="ExternalOutput").ap()
    d["xT"] = nc.dram_tensor("xT_scr", [128, 8, T], F32, kind="Internal").ap()
    d["yaT"] = nc.dram_tensor("yaT_scr", [128, 8, T], BF16, kind="Internal").ap()
    d["ybT"] = nc.dram_tensor("ybT_scr", [128, 8, T], BF16, kind="Internal").ap()
    d["hTs"] = nc.dram_tensor("hT_scr", [128, 8, T], BF16, kind="Internal").ap()
    d["cosT"] = nc.dram_tensor("cosT_scr", [128, T], F32, kind="Internal").ap()
    d["sinT"] = nc.dram_tensor("sinT_scr", [128, T], F32, kind="Internal").ap()
    if dbg is not None:
        d["dbg"] = nc.dram_tensor("dbg", [128, 8, T], F32, kind="ExternalOutput").ap()
    k.d = d
    phase_setup(k)
    done = False
    for l in range(n_layers):
        for stage in ("ffn1", "mixA", "mixB", "mixC", "ffn2"):
            last = (l == n_layers - 1 and stage == "ffn2")
            if stage == "ffn1":
                phase_ffn(k, l, 1, src_tok=False, dst_tok=False, src_ap=(d["x"] if l == 0 else None))
            elif stage == "mixA":
                phase_mixA(k, l)
            elif stage == "mixB":
                phase_mixB(k, l)
            elif stage == "mixC":
                phase_mixC(k, l)
            else:
                phase_ffn(k, l, 2, src_tok=False, dst_tok=False, dst_ap=(d["out"] if last else None))
            if stop is not None and (l, stage) == tuple(stop):
                done = True
                break
        if done:
            break
    if dbg is not None:
        al = k.al
        t = al.sb("dbgt", [128, 8, 512], F32)
        src = d["xT"]
        for i in range(T // 512):
            k.fw.dma("sync", t[:], src[:, :, i * 512:(i + 1) * 512], writes=[("dbgt", None)])
            k.fw.dma("sync", d["dbg"][:, :, i * 512:(i + 1) * 512], t[:], reads=[("dbgt", None)], writes=[("dbgd", None)])
    k.fw.finish()
    return nc


def _pk(w, kc=8):
    K_, N = w.shape
    return np.ascontiguousarray(w.reshape(K_ // 128, 128, N).transpose(1, 0, 2))


def _vec(v):
    return np.ascontiguousarray(v.reshape(-1, 128).T)


def make_consts():
    c = np.zeros((128, NCONST), np.float32)
    c[:, IDENT:IDENT + 128] = np.eye(128, dtype=np.float32)
    c[:, ONES:ONES + 128] = 1.0
    for p in range(128):
        for m in range(128):
            if p // 64 == m // 64:
                c[p, BLK + m] = 1.0
    for m in range(128):
        if (m % 64) < 32:
            c[m + 32, ROT + m] = -1.0
        else:
            c[m - 32, ROT + m] = 1.0
    for p in range(128):
        for t in range(64):
            if (p % 64) <= t:
                c[p, MASK + t] = 1.0
    for p in range(128):
        c[p, MASK2 + p:MASK2 + 128] = 1.0
    c[0, HSEL:HSEL + 64] = 1.0
    c[1, HSEL + 64:HSEL + 128] = 1.0
    for h in range(4):
        c[h, SEL + h * 128:SEL + (h + 1) * 128] = 1.0
    j = (np.arange(128) % 32).astype(np.float32)
    c[:, INVF] = (np.float32(10000.0) ** (-(2.0 * j) / np.float32(64.0))).astype(np.float32)
    return c


def prep_shared(inp):
    f = lambda a: np.asarray(a, dtype=np.float32)
    sh = {}
    sh["ada_w"] = np.stack([_pk(f(inp["ada_w"][l])) for l in range(L)])
    sh["f1_win"] = np.stack([_pk(f(inp["ffn1_w_in"][l])).reshape(128, -1) for l in range(L)])
    sh["f1_wout"] = np.stack([_pk(f(inp["ffn1_w_out"][l])).reshape(128, -1) for l in range(L)])
    sh["f2_win"] = np.stack([_pk(f(inp["ffn2_w_in"][l])).reshape(128, -1) for l in range(L)])
    sh["f2_wout"] = np.stack([_pk(f(inp["ffn2_w_out"][l])).reshape(128, -1) for l in range(L)])
    sh["mixw"] = np.stack([_pk(f(inp["mix_w_in"][l])) for l in range(L)])
    wka = []
    for l in range(L):
        ka = _pk(f(inp["mix_w_in"][l][:, 4104:4360]))
        ka = ka.reshape(128, 8, 4, 1, 64)
        wka.append(np.ascontiguousarray(np.broadcast_to(ka, (128, 8, 4, 2, 64)).reshape(128, 8, 512)))
    sh["wka_dup"] = np.stack(wka)
    def hw(w):
        w = f(w).reshape(4, 2, 128, 256)
        return np.ascontiguousarray(w.transpose(2, 1, 0, 3).reshape(128, 2, 1024))
    sh["wq"] = np.stack([hw(inp["m_wq"][l]) for l in range(L)])
    sh["wk"] = np.stack([hw(inp["m_wk"][l]) for l in range(L)])
    sh["proj_a"] = np.stack([_pk(f(inp["proj_a"][l])) for l in range(L)])
    sh["proj_b"] = np.stack([_pk(f(inp["proj_b"][l])) for l in range(L)])
    sh["merge_w"] = np.stack([_pk(f(inp["merge_w"][l])) for l in range(L)])
    sh["w_out"] = np.stack([_pk(f(inp["w_out"][l])) for l in range(L)])
    pp = np.zeros((128, L, NPP), np.float32)
    for l in range(L):
        pp[:, l, ADAB:ADAB + 72] = _vec(f(inp["ada_b"][l]))
        pp[:, l, N1:N1 + 8] = _vec(f(inp["ffn1_norm"][l]))
        pp[:, l, N2:N2 + 8] = _vec(f(inp["mix_norm"][l]))
        pp[:, l, N3:N3 + 8] = _vec(f(inp["ffn2_norm"][l]))
        cw = f(inp["m_conv_w"][l])
        for j in range(4):
            pp[:, l, CW + j:CW + 32:4] = _vec(cw[j])
        pp[:, l, CB:CB + 8] = _vec(f(inp["m_conv_b"][l]))
        pp[:, l, ONORM:ONORM + 8] = _vec(f(inp["m_out_norm"][l]))
        pp[:, l, SKIP:SKIP + 8] = _vec(f(inp["m_skip"][l]))
        pp[:, l, MB:MB + 16] = _vec(f(inp["merge_b"][l]))
        pp[:, l, QW] = np.concatenate([f(inp["a_q_norm"][l])] * 2)
        pp[:, l, KW] = np.concatenate([f(inp["a_k_norm"][l])] * 2)
        sk = f(inp["a_sinks"][l])
        for m in range(8):
            pp[0:64, l, SINK + m] = sk[2 * m]
            pp[64:128, l, SINK + m] = sk[2 * m + 1]
        for m in range(8):
            pp[0, l, SINK2 + m] = sk[2 * m]
            pp[1, l, SINK2 + m] = sk[2 * m + 1]
        gb = f(inp["m_gate_b"][l])
        pp[0:4, l, GBI] = gb[0:4]
        pp[0:4, l, GBF] = gb[4:8]
    sh["pp"] = pp
    sh["consts"] = make_consts()
    return sh


def make_in_maps(inp):
    sh = prep_shared(inp)
    x = np.asarray(inp["x"], dtype=np.float32)
    c = np.asarray(inp["c"], dtype=np.float32)
    pos = np.asarray(inp["positions"], dtype=np.int32)
    maps = []
    for core in range(NCORES):
        bs = slice(core * NBC, (core + 1) * NBC)
        m = dict(sh)
        m["x"] = np.ascontiguousarray(x[bs].reshape(T, 8, 128).transpose(2, 1, 0))
        m["cT"] = np.ascontiguousarray(c[bs].reshape(NBC, 8, 128).transpose(2, 1, 0))
        m["pos"] = np.ascontiguousarray(pos[bs])
        maps.append(m)
    return maps


def kernel(**inputs):
    maps = make_in_maps(inputs)
    nc = build_program()
    res = run_bass_kernel_spmd(nc, maps, core_ids=list(range(NCORES)))
    out = np.stack([np.ascontiguousarray(np.asarray(r["out"]).transpose(2, 1, 0)).reshape(NBC, S, D) for r in res.results])
    return out.reshape(NCORES * NBC, S, D).astype(np.float32)
```
